# Optimizing a Trainium2 kernel written in Bass

```python
import jax, jax.numpy as jnp
from jax import lax
import numpy as np

D_MODEL = 2048
BATCH = 4
SEQ = 8192
DEPTH = 1

GRID_W = 64
WIN_R = 8
WIN_C = 16
QCB = 16
NCB = GRID_W // QCB
KCW = QCB + WIN_C
N_ATT_HEADS = 8
ATT_HEAD_DIM = 128
ATT_WIDTH = N_ATT_HEADS * ATT_HEAD_DIM
N_DN_HEADS = 8
DN_HEAD_DIM = 128
DN_KEY_WIDTH = N_DN_HEADS * DN_HEAD_DIM
DN_VAL_WIDTH = N_DN_HEADS * DN_HEAD_DIM
DN_QKV_WIDTH = 2 * DN_KEY_WIDTH + DN_VAL_WIDTH
CONV_K = 5
CHUNK = 64
N_DIR = 2
IN_SPLITS = (3 * ATT_WIDTH, DN_QKV_WIDTH, DN_VAL_WIDTH, N_DIR * N_DN_HEADS, N_DIR * N_DN_HEADS, D_MODEL, D_MODEL)
IN_WIDTH = sum(IN_SPLITS)
D_FF = 4 * D_MODEL
RMS_EPS = 1e-6
L2_EPS = 1e-6
NEG_INF = -1e30

kernel_name = "hybrid_natten_gdn_encoder_block"


def _rmsnorm(x, w):
    xf = x.astype(jnp.float32)
    y = xf * lax.rsqrt(jnp.mean(xf * xf, axis=-1, keepdims=True) + RMS_EPS)
    return (y * w.astype(jnp.float32)).astype(x.dtype)


def _l2norm(x):
    return x * lax.rsqrt(jnp.sum(x * x, axis=-1, keepdims=True) + L2_EPS)


def _window_tables(wr):
    qcol = np.arange(GRID_W).reshape(NCB, QCB)
    cs = np.clip(qcol - WIN_C // 2, 0, GRID_W - WIN_C)
    kcs = np.clip(np.arange(NCB) * QCB - WIN_C // 2, 0, GRID_W - KCW)
    col_idx = kcs[:, None] + np.arange(KCW)[None, :]
    kk = np.arange(wr * KCW)
    key_row = kk // KCW
    key_col = col_idx[:, kk % KCW]
    kc = key_col[:, None, :]
    col_mask = (kc >= cs[..., None]) & (kc < cs[..., None] + WIN_C)
    dc_idx = np.clip(kc - qcol[..., None] + WIN_C - 1, 0, 2 * WIN_C - 2)
    return col_idx, key_row, col_mask, dc_idx


def _neighbourhood_attention(q, k, v, rpb):
    b, s, h, dh = q.shape
    rows = s // GRID_W
    wr = min(WIN_R, rows)
    col_idx, key_row, col_mask, dc_idx = _window_tables(wr)
    key_row = jnp.asarray(key_row, jnp.int32)
    dc_idx = jnp.asarray(dc_idx, jnp.int32)
    rpb_flat = rpb.reshape(h, -1).astype(jnp.float32)

    def grid(a):
        return a.reshape(b, rows, GRID_W, h, dh).transpose(1, 0, 3, 2, 4)

    qg, kg, vg = grid(q), grid(k), grid(v)

    def band(a, rs):
        a = lax.dynamic_slice_in_dim(a, rs, wr, axis=0)[:, :, :, col_idx]
        return a.transpose(1, 2, 3, 0, 4, 5).reshape(b, h, NCB, wr * KCW, dh)

    def row_step(args):
        r, q_r = args
        rs = jnp.clip(r - wr // 2, 0, rows - wr)
        kb, vb = band(kg, rs), band(vg, rs)
        qb = q_r.reshape(b, h, NCB, QCB, dh)
        sc = jnp.einsum('bhjqd,bhjkd->bhjqk', qb, kb, preferred_element_type=jnp.float32) * (dh ** -0.5)
        dr = rs - r + key_row + (WIN_R - 1)
        bias = rpb_flat[:, dr[None, None, :] * (2 * WIN_C - 1) + dc_idx]
        sc = jnp.where(col_mask, sc + bias[None], NEG_INF)
        p = jax.nn.softmax(sc, axis=-1).astype(vb.dtype)
        return jnp.einsum('bhjqk,bhjkd->bhjqd', p, vb).reshape(b, h, GRID_W, dh)

    o = lax.map(row_step, (jnp.arange(rows, dtype=jnp.int32), qg))
    return o.transpose(1, 0, 3, 2, 4).reshape(b, s, h * dh)


def _centred_depthwise_conv(x, w):
    c = x.shape[-1]
    pad = CONV_K // 2
    return lax.conv_general_dilated(x, w.reshape(CONV_K, 1, c).astype(x.dtype), (1,), [(pad, pad)],
                                    dimension_numbers=('NWC', 'WIO', 'NWC'), feature_group_count=c)


def _chunk_gated_delta(q, k, v, g, beta):
    b, t, h, dk = q.shape
    dv = v.shape[-1]
    n = t // CHUNK

    def chunks(a):
        return a.reshape(b, n, CHUNK, h, -1).transpose(1, 0, 3, 2, 4)

    q = chunks(q) * (dk ** -0.5)
    k, v = chunks(k), chunks(v)
    g = chunks(g[..., None])[..., 0]
    beta = chunks(beta[..., None])[..., 0]
    gc = jnp.cumsum(g, axis=-1)
    incl = jnp.tril(jnp.ones((CHUNK, CHUNK), dtype=bool))
    strict = jnp.tril(jnp.ones((CHUNK, CHUNK), dtype=bool), -1)
    decay = jnp.exp(jnp.where(incl, gc[..., :, None] - gc[..., None, :], -jnp.inf))
    kb = k * beta[..., None]
    a_mat = jnp.where(strict, jnp.einsum('nbhid,nbhjd->nbhij', kb, k) * decay, 0.0)
    rhs = jnp.concatenate([v * beta[..., None], kb * jnp.exp(gc)[..., None]], axis=-1)
    sol = lax.linalg.triangular_solve(a_mat, rhs, left_side=True, lower=True, unit_diagonal=True)
    u, w = sol[..., :dv], sol[..., dv:]
    qk = jnp.einsum('nbhid,nbhjd->nbhij', q, k) * decay
    qg = q * jnp.exp(gc)[..., None]
    kg = k * jnp.exp(gc[..., -1:] - gc)[..., None]
    glast = jnp.exp(gc[..., -1])

    def step(state, inp):
        qg_c, kg_c, u_c, w_c, qk_c, gl_c = inp
        v_new = u_c - jnp.einsum('bhck,bhkv->bhcv', w_c, state)
        o = jnp.einsum('bhck,bhkv->bhcv', qg_c, state) + jnp.einsum('bhij,bhjv->bhiv', qk_c, v_new)
        state = state * gl_c[..., None, None] + jnp.einsum('bhck,bhcv->bhkv', kg_c, v_new)
        return state, o

    s0 = jnp.zeros((b, h, dk, dv), jnp.float32)
    _, o = lax.scan(step, s0, (qg, kg, u, w, qk, glast))
    return o.transpose(1, 0, 3, 2, 4).reshape(b, t, h, dv)


def _bidirectional_gated_deltanet(qkv_raw, z, beta_raw, alpha_raw, conv_w, a_log, dt_bias, onorm_w):
    b, s, _ = qkv_raw.shape
    qkv = jax.nn.silu(_centred_depthwise_conv(qkv_raw, conv_w)).astype(jnp.float32)
    q, k, v = jnp.split(qkv, [DN_KEY_WIDTH, 2 * DN_KEY_WIDTH], axis=-1)
    q = _l2norm(q.reshape(b, s, N_DN_HEADS, DN_HEAD_DIM))
    k = _l2norm(k.reshape(b, s, N_DN_HEADS, DN_HEAD_DIM))
    v = v.reshape(b, s, N_DN_HEADS, DN_HEAD_DIM)
    beta = jax.nn.sigmoid(beta_raw.astype(jnp.float32)).reshape(b, s, N_DIR, N_DN_HEADS)
    g = -jnp.exp(a_log.astype(jnp.float32)) * jax.nn.softplus(
        alpha_raw.astype(jnp.float32).reshape(b, s, N_DIR, N_DN_HEADS) + dt_bias.astype(jnp.float32))
    q2 = jnp.concatenate([q, q[:, ::-1]], axis=0)
    k2 = jnp.concatenate([k, k[:, ::-1]], axis=0)
    v2 = jnp.concatenate([v, v[:, ::-1]], axis=0)
    g2 = jnp.concatenate([g[:, :, 0], g[:, ::-1, 1]], axis=0)
    b2 = jnp.concatenate([beta[:, :, 0], beta[:, ::-1, 1]], axis=0)
    o2 = _chunk_gated_delta(q2, k2, v2, g2, b2)
    o = o2[:b] + o2[b:, ::-1]
    zh = z.astype(jnp.float32).reshape(b, s, N_DN_HEADS, DN_HEAD_DIM)
    o = _rmsnorm(o, onorm_w) * jax.nn.silu(zh)
    return o.reshape(b, s, DN_VAL_WIDTH).astype(qkv_raw.dtype)


def setup_inputs(seed: int = 0) -> dict:
    key = jax.random.key(seed)
    ks = jax.random.split(key, 16)
    f32 = jnp.float32
    nrm = lambda kk, shape, scale: jax.random.normal(kk, shape, f32) * scale
    dt = jnp.exp(jax.random.uniform(ks[5], (N_DIR, N_DN_HEADS), f32, np.log(1e-3), np.log(1e-1)))
    return {
        "x": nrm(ks[0], (BATCH, SEQ, D_MODEL), 1.0),
        "norm_mix_w": 1.0 + nrm(ks[1], (D_MODEL,), 0.02),
        "w_in": nrm(ks[2], (D_MODEL, IN_WIDTH), D_MODEL ** -0.5),
        "conv_w": nrm(ks[3], (CONV_K, DN_QKV_WIDTH), CONV_K ** -0.5),
        "a_log": jnp.log(jax.random.uniform(ks[4], (N_DIR, N_DN_HEADS), f32, 1.0, 16.0)),
        "dt_bias": dt + jnp.log(-jnp.expm1(-dt)),
        "onorm_w": 1.0 + nrm(ks[6], (DN_HEAD_DIM,), 0.02),
        "rpb": nrm(ks[7], (N_ATT_HEADS, 2 * WIN_R - 1, 2 * WIN_C - 1), 0.1),
        "w_proj_a": nrm(ks[8], (ATT_WIDTH, D_MODEL), ATT_WIDTH ** -0.5),
        "w_proj_b": nrm(ks[9], (DN_VAL_WIDTH, D_MODEL), DN_VAL_WIDTH ** -0.5),
        "w_out": nrm(ks[10], (D_MODEL, D_MODEL), D_MODEL ** -0.5),
        "norm_mlp_w": 1.0 + nrm(ks[11], (D_MODEL,), 0.02),
        "w_mlp_up": nrm(ks[12], (D_MODEL, D_FF), D_MODEL ** -0.5),
        "w_mlp_down": nrm(ks[13], (D_FF, D_MODEL), D_FF ** -0.5),
        "norm_final_w": 1.0 + nrm(ks[14], (D_MODEL,), 0.02),
    }


def reference(x, norm_mix_w, w_in, conv_w, a_log, dt_bias, onorm_w, rpb, w_proj_a, w_proj_b,
              w_out, norm_mlp_w, w_mlp_up, w_mlp_down, norm_final_w):
    b, s, _ = x.shape
    h = x
    for _ in range(DEPTH):
        xn = _rmsnorm(h, norm_mix_w)
        proj = xn @ w_in
        att_qkv, dn_qkv, dn_z, dn_beta, dn_alpha, gate_a, gate_b = jnp.split(
            proj, list(np.cumsum(IN_SPLITS)[:-1]), axis=-1)
        aq, ak, av = jnp.split(att_qkv.reshape(b, s, 3, N_ATT_HEADS, ATT_HEAD_DIM), 3, axis=2)
        y_a = _neighbourhood_attention(aq[:, :, 0], ak[:, :, 0], av[:, :, 0], rpb)
        y_b = _bidirectional_gated_deltanet(dn_qkv, dn_z, dn_beta, dn_alpha, conv_w, a_log, dt_bias, onorm_w)
        mixed = jax.nn.sigmoid(gate_a) * (y_a @ w_proj_a) + jax.nn.sigmoid(gate_b) * (y_b @ w_proj_b)
        h = h + mixed @ w_out
        hn = _rmsnorm(h, norm_mlp_w)
        h = h + jnp.square(jax.nn.relu(hn @ w_mlp_up)) @ w_mlp_down
    return _rmsnorm(h, norm_final_w)
```

```python
import contextlib
import numpy as np
import concourse.bass as bass
import concourse.mybir as mybir
from concourse.bass_utils import run_bass_kernel_spmd

F32 = mybir.dt.float32
BF16 = mybir.dt.bfloat16
AF = mybir.ActivationFunctionType
ALU = mybir.AluOpType
AX = mybir.AxisListType

D = 2048
NH = 8
HD = 128
IN_W = 11296
DFF = 8192
GRID_W = 64
C_ATT_Q, C_ATT_K, C_ATT_V = 0, 1024, 2048
C_DN_Q, C_DN_K, C_DN_V = 3072, 4096, 5120
C_Z, C_BETA, C_ALPHA, C_GA, C_GB = 6144, 7168, 7184, 7200, 9248
RMS_EPS = 1e-6
NEG = -30000.0

ENGS = ("pe", "act", "dve", "pool", "sp")


class Buf:
    __slots__ = ("name", "w", "r", "sem", "cnt", "last_dma", "excl")

    def __init__(self, name="", excl=False):
        self.name = name
        self.excl = excl
        self.w = {}
        self.r = {}
        self.sem = None
        self.cnt = 0
        self.last_dma = None


class Op:
    __slots__ = ("eng", "fn", "deps", "need", "tok", "chan")


class Prog:
    def __init__(self, nc):
        self.nc = nc
        self.ops = {e: [] for e in ENGS}
        self.allops = []
        self.chans = []
        self.pending_dma = []
        self.last = {e: None for e in ENGS}
        self.snap = []

    def op(self, eng, fn, r=(), w=(), chan=None, extra=()):
        o = Op()
        o.eng = eng
        o.fn = fn
        o.need = False
        o.tok = None
        o.chan = chan
        deps = {}
        for d in extra:
            deps[id(d)] = d
        for b in r:
            for d in b.w.values():
                deps[id(d)] = d
            if b.excl:
                for k, d in b.r.items():
                    if k != eng:
                        deps[id(d)] = d
        for b in w:
            for d in b.r.values():
                deps[id(d)] = d
            for d in b.w.values():
                deps[id(d)] = d
        if chan is not None:
            if chan.last_dma is not None:
                deps[id(chan.last_dma)] = chan.last_dma
            chan.last_dma = o
            if chan.sem is None:
                chan.sem = True
                self.chans.append(chan)
            self.pending_dma.append(o)
        key = id(o) if chan is not None else eng
        for b in r:
            b.r[key] = o
        for b in w:
            b.w = {key: o}
            b.r = {}
        if eng == "pe" and chan is None:
            deps = {k: d for k, d in deps.items() if not (d.eng == "pe" and d.chan is None)}
        o.deps = [d for d in deps.values() if d.fn is not None]
        if fn is not None and fn.__closure__:
            self.snap.append((fn, [id(c.cell_contents) for c in fn.__closure__]))
        self.ops[eng].append(o)
        self.allops.append(o)
        if chan is None and fn is not None:
            self.last[eng] = o
        return o

    def dma(self, eng, out, in_, r=(), w=(), chan=None, **kw):
        assert chan is not None
        return self.op(eng, lambda e: e.dma_start(out=out, in_=in_, **kw), r=r, w=w, chan=chan)

    def barrier(self):
        deps = [o for o in self.last.values() if o is not None] + self.pending_dma
        self.pending_dma = []
        for e in ENGS:
            self.op(e, None, extra=deps)

    def emit(self, st, final_waits=()):
        nc = self.nc
        bad = set()
        for fn, ids in self.snap:
            for name, c, i in zip(fn.__code__.co_freevars, fn.__closure__, ids):
                if id(c.cell_contents) != i:
                    bad.add((fn.__code__.co_firstlineno, name))
        if bad:
            raise RuntimeError("late-binding closure bug(s): %s" % sorted(bad))
        for o in self.allops:
            for d in o.deps:
                d.need = True
        for o in final_waits:
            o.need = True
        cnt = {e: 0 for e in ENGS}
        for o in self.allops:
            if o.fn is None:
                continue
            if o.chan is None:
                if o.need:
                    cnt[o.eng] += 1
                    o.tok = (o.eng, cnt[o.eng])
            else:
                o.chan.cnt += 16
                o.tok = (o.chan, o.chan.cnt)
        esem = {e: st.enter_context(nc.semaphore("s_" + e)) for e in ENGS}
        for i, c in enumerate(self.chans):
            c.sem = st.enter_context(nc.semaphore("d%d" % i))
        block = st.enter_context(nc.Block())

        def run(ename, eng, extra=()):
            seen = {}

            def wait(d):
                k, v = d.tok
                sem = esem[k] if isinstance(k, str) else k.sem
                sk = id(sem)
                if seen.get(sk, 0) < v:
                    eng.wait_ge(sem, v)
                    seen[sk] = v

            for o in self.ops[ename]:
                for d in o.deps:
                    wait(d)
                if o.fn is None:
                    continue
                ins = o.fn(eng)
                if o.chan is not None:
                    ins.then_inc(o.chan.sem, 16)
                elif o.need:
                    ins.then_inc(esem[ename], 1)
            for d in extra:
                wait(d)

        @block.tensor
        def _(e):
            run("pe", e)

        @block.scalar
        def _(e):
            run("act", e)

        @block.vector
        def _(e):
            run("dve", e)

        @block.gpsimd
        def _(e):
            run("pool", e)

        @block.sync
        def _(e):
            run("sp", e, extra=final_waits)

        return cnt, len(self.chans)


class Arena:
    def __init__(self, t, nwords):
        self.t = t
        self.n = nwords
        self.off = 0

    def reset(self):
        self.off = 0

    def alloc(self, free_shape, dt, name=""):
        n = int(np.prod(free_shape))
        words = n if dt == F32 else (n + 1) // 2
        a = self.off
        self.off += words
        assert self.off <= self.n, ("SBUF arena overflow", name, self.off, self.n)
        ap = self.t[:, a:a + words]
        if dt != F32:
            ap = ap.bitcast(dt)
        if len(free_shape) == 2:
            ap = ap.rearrange("p (a b) -> p a b", b=free_shape[1])
        elif len(free_shape) == 3:
            ap = ap.rearrange("p (a b c) -> p a b c", b=free_shape[1], c=free_shape[2])
        return ap


class Ring:
    def __init__(self, arena, n, free_shape, dt, name):
        self.t = [arena.alloc(free_shape, dt, name) for _ in range(n)]
        self.b = [Buf("%s%d" % (name, i)) for i in range(n)]
        self.i = 0
        self.n = n

    def get(self):
        i = self.i % self.n
        self.i += 1
        return self.t[i], self.b[i]


class Cfg:
    def __init__(self, S, debug=False, phases=(1, 2, 3, 4)):
        self.S = S
        self.TOK = S // 2
        self.NU = self.TOK // 128
        self.NST = self.TOK // 512
        self.HALO = 256
        self.debug = debug
        self.phases = phases


def build(cfg):
    S, TOK, NU, NST = cfg.S, cfg.TOK, cfg.NU, cfg.NST
    nc = bass.Bass("TRN2", target_bir_lowering=False)
    dbg_kind = "ExternalOutput" if cfg.debug else "Internal"

    def din(name, shape, dt=F32):
        return nc.dram_tensor(name, list(shape), dt, kind="ExternalInput")

    def dscr(name, shape, dt, dbg=False):
        return nc.dram_tensor(name, list(shape), dt, kind=(dbg_kind if dbg else "Internal"))

    xs = din("xs", [S, D])
    norm_mix_w = din("norm_mix_w", [D])
    w_in = din("w_in", [D, IN_W])
    w_ba = din("w_ba", [D, 32])
    conv_w = din("conv_w", [128, 24, 5])
    a_log = din("a_log", [2, 8])
    dt_bias = din("dt_bias", [2, 8])
    onorm_w = din("onorm_w", [128])
    w_proj_a = din("w_proj_a", [1024, D])
    w_proj_b = din("w_proj_b", [1024, D])
    w_out = din("w_out", [D, D])
    norm_mlp_w = din("norm_mlp_w", [D])
    w_mlp_up = din("w_mlp_up", [D, DFF])
    w_mlp_down = din("w_mlp_down", [DFF, D])
    norm_final_w = din("norm_final_w", [D])
    out = nc.dram_tensor("out", [TOK, D], F32, kind="ExternalOutput")
    NP = TOK // 128
    NT = NP + 2
    att_t2 = din("att_t2", [128, NH, 7, 128])
    att_rmv = din("att_rmv", [2, NP, 7, 128])
    att_ind = din("att_ind", [2, 128])
    gmask = din("gmask", [128, 9, 128])
    gsel = din("gsel", [8, 8, 128])
    ofw = dscr("ofw", [TOK, 1024], F32)
    dnc = dscr("dnc", [24, 128, S], BF16)

    wb_in = dscr("wb_in", [D, IN_W], BF16)
    wb_ba = dscr("wb_ba", [D, 32], BF16)
    wb_pa = dscr("wb_pa", [1024, D], BF16)
    wb_pb = dscr("wb_pb", [1024, D], BF16)
    wb_out = dscr("wb_out", [D, D], BF16)
    wb_up = dscr("wb_up", [D, DFF], BF16)
    wb_dn = dscr("wb_dn", [DFF, D], BF16)
    TK = TOK + cfg.HALO
    qT = dscr("qT", [NH, 128, TOK], BF16, dbg=True)
    kT = dscr("kT", [NH, 128, TK], BF16, dbg=True)
    vA = dscr("vA", [TK, 1024], BF16, dbg=True)
    RAWW = S + 4
    dnraw = dscr("dnraw", [24, 128, RAWW], F32, dbg=True)
    zs = dscr("zs", [TOK, 1024], F32, dbg=True)
    ba = dscr("ba", [S, 32], F32, dbg=True)
    gA = dscr("gA", [16, 128, TOK], BF16, dbg=True)
    gB = dscr("gB", [16, 128, TOK], BF16, dbg=True)

    inj = getattr(cfg, "inject", False)
    ykind = "ExternalInput" if inj else dbg_kind
    yaT = nc.dram_tensor("yaT", [NH, 128, TOK], BF16, kind=(dbg_kind if getattr(cfg, "inject_b_only", False) else ykind))
    ybT = nc.dram_tensor("ybT", [NH, 128, TOK], BF16, kind=ykind)

    st = contextlib.ExitStack()
    with st:
        NW = 50 * 1024
        arena_t = st.enter_context(nc.sbuf_tensor("arena", [128, NW], F32))
        AR = Arena(arena_t, NW)
        psum = [st.enter_context(nc.psum_tensor("ps%d" % i, [128, 512], F32)) for i in range(8)]
        psb = [Buf("ps%d" % i, excl=True) for i in range(8)]
        P = Prog(nc)

        dram_bufs = {}

        def DB(key):
            if key not in dram_bufs:
                dram_bufs[key] = Buf(str(key))
            return dram_bufs[key]

        castch = [Buf("cast%d" % i) for i in range(4)]
        ci = [0]

        def cast_weight(src, dst, rows, key):
            for r0 in range(0, rows, 128):
                ch = castch[ci[0] % 4]
                ci[0] += 1
                P.dma("pool", dst[r0:r0 + 128, :], src[r0:r0 + 128, :], w=[DB((key, r0 // 128))], chan=ch)

        cast_weight(w_ba, wb_ba, D, "wb_ba")
        cast_weight(w_in, wb_in, D, "wb_in")

        def phase1():
            AR.reset()
            wbc = AR.alloc([D], F32, "wbc")
            wbc_b = Buf("wbc")
            P.dma("sp", wbc, bass.AP(tensor=norm_mix_w, offset=0, ap=[[0, 128], [1, D]]), w=[wbc_b], chan=wbc_b)
            ident = AR.alloc([128], BF16, "ident")
            identf = AR.alloc([128], F32, "identf")
            ident_b = Buf("ident")
            P.op("pool", lambda e: e.memset(identf, 0.0), w=[ident_b])
            P.op("pool", lambda e: e.affine_select(out=identf, in_=identf, pattern=[[-1, 128]],
                                                   compare_op=ALU.not_equal, fill=1.0, base=0,
                                                   channel_multiplier=1), r=[ident_b], w=[ident_b])
            P.op("dve", lambda e: e.tensor_copy(out=ident, in_=identf), r=[ident_b], w=[ident_b])
            zero_t = AR.alloc([4], F32, "zero")
            zero_b = Buf("zero")
            P.op("pool", lambda e: e.memset(zero_t, 0.0), w=[zero_b])
            for c in range(24):
                P.dma("sp", dnraw[c, :, 0:2], zero_t[:, 0:2], r=[zero_b], w=[DB(("rawpad", c, 0))], chan=zero_b)
                P.dma("sp", dnraw[c, :, S + 2:S + 4], zero_t[:, 2:4], r=[zero_b], w=[DB(("rawpad", c, 1))], chan=zero_b)

            xring = Ring(AR, 3, [D], F32, "x")
            junk = AR.alloc([D], BF16, "junk")
            junk_b = Buf("junk")
            ssr = Ring(AR, 4, [2], F32, "ss")
            xnbr = Ring(AR, 2, [D], BF16, "xnb")
            xnTr = [AR.alloc([16, 512], BF16, "xnT%d" % i) for i in range(2)]
            xnTb = [[Buf("xnT%d_%d" % (i, j)) for j in range(4)] for i in range(2)]
            wring = Ring(AR, 3, [16, 512], BF16, "wslab")
            stg = Ring(AR, 6, [512], F32, "stg")
            evi = [0]
            mmi = [0]

            def evac_engine():
                evi[0] += 1
                return "act" if evi[0] % 2 else "dve"

            def mmbank():
                i = mmi[0] % 6
                mmi[0] += 1
                return psum[i], psb[i]

            def evac(kind, dst, src, rb, wbuf):
                if kind == "copy":
                    eng = evac_engine()
                    if eng == "act":
                        P.op("act", lambda e: e.copy(out=dst, in_=src), r=rb, w=wbuf)
                    else:
                        P.op("dve", lambda e: e.tensor_copy(out=dst, in_=src), r=rb, w=wbuf)
                elif kind == "sigmoid":
                    P.op("act", lambda e: e.activation(out=dst, in_=src, func=AF.Sigmoid), r=rb, w=wbuf)
                elif kind == "silu":
                    P.op("act", lambda e: e.activation(out=dst, in_=src, func=AF.Silu), r=rb, w=wbuf)

            def load_wslab(col0, ncols):
                wt, wbf = wring.get()
                if col0 == C_BETA:
                    src = wb_ba.rearrange("(c p) n -> p c n", p=128)
                    rk = "wb_ba"
                else:
                    src = wb_in.rearrange("(c p) n -> p c n", p=128)[:, :, col0:col0 + ncols]
                    rk = "wb_in"
                P.dma("sp", wt[:, :, 0:ncols], src, r=[DB((rk, c)) for c in range(16)], w=[wbf], chan=wbf)
                return wt, wbf

            def fm_group(xi, col0, ncols, ntok, kind, odt, dst_fn, dkey):
                wt, wbf = load_wslab(col0, ncols)
                xT = xnTr[xi]
                for c in range(ncols // 128):
                    pt, pb = mmbank()
                    for k in range(16):
                        P.op("pe", lambda e, k=k, c=c, pt=pt: e.matmul(pt[:, 0:ntok], lhsT=wt[:, k, c * 128:(c + 1) * 128],
                                                                    rhs=xT[:, k, 0:ntok], start=(k == 0), stop=(k == 15)),
                             r=[wbf] + xnTb[xi], w=[pb])
                    sg, sgb = stg.get()
                    dst = sg[:, 0:ntok] if odt == F32 else sg.bitcast(BF16)[:, 0:ntok]
                    evac(kind, dst, pt[:, 0:ntok], [pb], [sgb])
                    P.dma("pool", dst_fn(c), dst, r=[sgb], w=[DB((dkey, col0 // 128 + c, "fm"))], chan=sgb)

            def tm_group(xi, col0, ncols, nsub, kind, odt, dst_fn, dkey):
                wt, wbf = load_wslab(col0, ncols)
                xT = xnTr[xi]
                for j in range(nsub):
                    pt, pb = mmbank()
                    for k in range(16):
                        P.op("pe", lambda e, k=k, j=j, pt=pt: e.matmul(pt[:, 0:ncols], lhsT=xT[:, k, j * 128:(j + 1) * 128],
                                                                    rhs=wt[:, k, 0:ncols], start=(k == 0), stop=(k == 15)),
                             r=[wbf, xnTb[xi][j]], w=[pb])
                    sg, sgb = stg.get()
                    dst = sg[:, 0:ncols] if odt == F32 else sg.bitcast(BF16)[:, 0:ncols]
                    evac(kind, dst, pt[:, 0:ncols], [pb], [sgb])
                    P.dma("pool", dst_fn(j), dst, r=[sgb], w=[DB((dkey, j, col0, "tm"))], chan=sgb)

            for sti in range(2 * NST):
                xi = sti % 2
                t0 = sti * 512
                own = sti < NST
                first_oth = sti == NST
                nsub = 4
                for j in range(nsub):
                    xt, xb_ = xring.get()
                    P.dma("sp", xt, xs[t0 + j * 128:t0 + (j + 1) * 128, :], w=[xb_], chan=xb_)
                    ss, ssb = ssr.get()
                    P.op("act", lambda e, xt=xt, ss=ss: e.activation(out=junk, in_=xt, func=AF.Square, accum_out=ss[:, 0:1]),
                         r=[xb_], w=[junk_b, ssb])
                    P.op("dve", lambda e, ss=ss: e.tensor_scalar(out=ss[:, 1:2], in0=ss[:, 0:1], scalar1=1.0 / D, scalar2=RMS_EPS,
                                                                 op0=ALU.mult, op1=ALU.add), r=[ssb], w=[ssb])
                    P.op("act", lambda e, ss=ss: e.sqrt(out=ss[:, 1:2], in_=ss[:, 1:2]), r=[ssb], w=[ssb])
                    P.op("dve", lambda e, ss=ss: e.reciprocal(out=ss[:, 1:2], in_=ss[:, 1:2]), r=[ssb], w=[ssb])
                    xnb, xnbb = xnbr.get()
                    P.op("dve", lambda e, xt=xt, ss=ss, xnb=xnb: e.scalar_tensor_tensor(out=xnb, in0=xt, scalar=ss[:, 1:2], in1=wbc,
                                                                                         op0=ALU.mult, op1=ALU.mult),
                         r=[xb_, ssb, wbc_b], w=[xnbb])
                    for hb in range(2):
                        tp = psum[6 + hb].bitcast(BF16)
                        tpb = psb[6 + hb]
                        for kk in range(8):
                            k = hb * 8 + kk
                            P.op("pe", lambda e, tp=tp, kk=kk, k=k, xnb=xnb: e.transpose(out=tp[:, kk * 128:(kk + 1) * 128],
                                                                                       in_=xnb[:, k * 128:(k + 1) * 128], identity=ident),
                                 r=[xnbb, ident_b], w=[tpb])
                        dst = xnTr[xi][:, hb * 8:(hb + 1) * 8, j * 128:(j + 1) * 128]
                        src = tp.rearrange("p (a b) -> p a b", b=128)
                        if hb == 0:
                            P.op("act", lambda e, dst=dst, src=src: e.copy(out=dst, in_=src), r=[tpb], w=[xnTb[xi][j]])
                        else:
                            P.op("dve", lambda e, dst=dst, src=src: e.tensor_copy(out=dst, in_=src), r=[tpb], w=[xnTb[xi][j]])
                if own:
                    for g in range(2):
                        fm_group(xi, C_ATT_Q + g * 512, 512, 512, "copy", BF16,
                                 lambda c, g=g: qT[g * 4 + c, :, t0:t0 + 512], "qT%d" % sti)
                if own or first_oth:
                    ntk = 512 if own else cfg.HALO
                    for g in range(2):
                        fm_group(xi, C_ATT_K + g * 512, 512, ntk, "copy", BF16,
                                 lambda c, g=g, ntk=ntk: kT[g * 4 + c, :, t0:t0 + ntk], "kT%d" % sti)
                    for g in range(2):
                        tm_group(xi, C_ATT_V + g * 512, 512, ntk // 128, "copy", BF16,
                                 lambda j, g=g: vA[t0 + j * 128:t0 + (j + 1) * 128, g * 512:(g + 1) * 512], "vA%d" % sti)
                    ntq = 512 if own else 128
                    for g in range(2):
                        fm_group(xi, C_DN_Q + g * 512, 512, ntq, "copy", F32,
                                 lambda c, g=g, ntq=ntq: dnraw[g * 4 + c, :, 2 + t0:2 + t0 + ntq], "dnq%d" % sti)
                for g in range(4):
                    fm_group(xi, C_DN_K + g * 512, 512, 512, "copy", F32,
                             lambda c, g=g: dnraw[8 + g * 4 + c, :, 2 + t0:2 + t0 + 512], "dnkv%d" % sti)
                tm_group(xi, C_BETA, 32, 4, "copy", F32,
                         lambda j: ba[t0 + j * 128:t0 + (j + 1) * 128, :], "ba%d" % sti)
                if own:
                    for g in range(2):
                        tm_group(xi, C_Z + g * 512, 512, 4, "silu", F32,
                                 lambda j, g=g: zs[t0 + j * 128:t0 + (j + 1) * 128, g * 512:(g + 1) * 512], "zs%d" % sti)
                    for g in range(4):
                        fm_group(xi, C_GA + g * 512, 512, 512, "sigmoid", BF16,
                                 lambda c, g=g: gA[g * 4 + c, :, t0:t0 + 512], "gA%d" % sti)
                    for g in range(4):
                        fm_group(xi, C_GB + g * 512, 512, 512, "sigmoid", BF16,
                                 lambda c, g=g: gB[g * 4 + c, :, t0:t0 + 512], "gB%d" % sti)


        def phase4():
            AR.reset()
            wmlp = AR.alloc([D], F32, "wmlp")
            wfin = AR.alloc([D], F32, "wfin")
            cst_b = Buf("cst4")
            P.dma("sp", wmlp, bass.AP(tensor=norm_mlp_w, offset=0, ap=[[0, 128], [1, D]]), w=[cst_b], chan=cst_b)
            P.dma("sp", wfin, bass.AP(tensor=norm_final_w, offset=0, ap=[[0, 128], [1, D]]), w=[cst_b], chan=cst_b)
            ident = AR.alloc([128], BF16, "ident")
            identf = AR.alloc([128], F32, "identf")
            ident_b = Buf("ident4")
            P.op("pool", lambda e: e.memset(identf, 0.0), w=[ident_b])
            P.op("pool", lambda e: e.affine_select(out=identf, in_=identf, pattern=[[-1, 128]],
                                                   compare_op=ALU.not_equal, fill=1.0, base=0,
                                                   channel_multiplier=1), r=[ident_b], w=[ident_b])
            P.op("dve", lambda e: e.tensor_copy(out=ident, in_=identf), r=[ident_b], w=[ident_b])
            aT = AR.alloc([64, 512], BF16, "aT")
            aT_b = Buf("aT")
            base = aT.rearrange("p a b -> p (a b)")
            ya_t = base[:, 0:4096].rearrange("p (a b) -> p a b", b=512)
            yb_t = base[:, 4096:8192].rearrange("p (a b) -> p a b", b=512)
            gA_t = base[:, 8192:16384].rearrange("p (a b) -> p a b", b=512)
            gB_t = base[:, 16384:24576].rearrange("p (a b) -> p a b", b=512)
            in_b = [Buf("ya_t"), Buf("yb_t"), Buf("gA_t"), Buf("gB_t")]
            mixT = AR.alloc([16, 512], BF16, "mixT")
            hnT = mixT
            mix_b = Buf("mixT")
            hT_b = [Buf("hnT%d" % j) for j in range(4)]
            ht = [AR.alloc([D], F32, "h%d" % j) for j in range(4)]
            h_b = [Buf("h%d" % j) for j in range(4)]
            hnr = Ring(AR, 2, [D], BF16, "hn")
            junk = AR.alloc([D], BF16, "junk4")
            junk_b = Buf("junk4")
            ssr = Ring(AR, 4, [2], F32, "ss4")
            wring = Ring(AR, 2, [16, 512], BF16, "wslab4")
            tmpr = Ring(AR, 3, [512], F32, "tmp4")
            mmi = [0]

            def mmbank():
                i = mmi[0] % 6
                mmi[0] += 1
                return psum[i], psb[i]

            def rstd(src_t, src_b):
                ss, ssb = ssr.get()
                P.op("act", lambda e: e.activation(out=junk, in_=src_t, func=AF.Square, accum_out=ss[:, 0:1]),
                     r=[src_b], w=[junk_b, ssb])
                P.op("dve", lambda e: e.tensor_scalar(out=ss[:, 1:2], in0=ss[:, 0:1], scalar1=1.0 / D, scalar2=RMS_EPS,
                                                      op0=ALU.mult, op1=ALU.add), r=[ssb], w=[ssb])
                P.op("act", lambda e: e.sqrt(out=ss[:, 1:2], in_=ss[:, 1:2]), r=[ssb], w=[ssb])
                P.op("dve", lambda e: e.reciprocal(out=ss[:, 1:2], in_=ss[:, 1:2]), r=[ssb], w=[ssb])
                return ss, ssb

            for tt in range(NST):
                t0 = tt * 512
                srcs = [yaT, ybT, gA, gB]
                dsts = [ya_t, yb_t, gA_t, gB_t]
                keys = ["yaT", "ybT", "gA", "gB"]
                for i in range(4):
                    nchunk = 8 if i < 2 else 16
                    P.dma("sp", dsts[i], srcs[i][:, :, t0:t0 + 512].rearrange("c p t -> p c t"),
                          r=[DB((keys[i], "all"))], w=[in_b[i], aT_b], chan=in_b[i])
                for j in range(4):
                    P.dma("sp", ht[j], xs[t0 + j * 128:t0 + (j + 1) * 128, :], w=[h_b[j]], chan=h_b[j])
                for mg in range(4):
                    wa, wab = wring.get()
                    P.dma("sp", wa[:, 0:8, :], wb_pa.rearrange("(c p) n -> p c n", p=128)[:, :, mg * 512:(mg + 1) * 512],
                          r=[DB(("wb_pa", c)) for c in range(8)], w=[wab], chan=wab)
                    P.dma("sp", wa[:, 8:16, :], wb_pb.rearrange("(c p) n -> p c n", p=128)[:, :, mg * 512:(mg + 1) * 512],
                          r=[DB(("wb_pb", c)) for c in range(8)], w=[wab], chan=wab)
                    for mm in range(4):
                        m = mg * 4 + mm
                        pa, pab = mmbank()
                        for k in range(8):
                            P.op("pe", lambda e, pa=pa, k=k, mm=mm, wa=wa: e.matmul(pa[:, :], lhsT=wa[:, k, mm * 128:(mm + 1) * 128], rhs=ya_t[:, k, :],
                                                                                 start=(k == 0), stop=(k == 7)), r=[wab, in_b[0]], w=[pab])
                        pb2, pbb = mmbank()
                        for k in range(8):
                            P.op("pe", lambda e, pb2=pb2, k=k, mm=mm, wa=wa: e.matmul(pb2[:, :], lhsT=wa[:, 8 + k, mm * 128:(mm + 1) * 128], rhs=yb_t[:, k, :],
                                                                                   start=(k == 0), stop=(k == 7)), r=[wab, in_b[1]], w=[pbb])
                        t1, t1b = tmpr.get()
                        t2, t2b = tmpr.get()
                        P.op("dve", lambda e, t1=t1, pa=pa, m=m: e.tensor_tensor(out=t1, in0=pa[:, :], in1=gA_t[:, m, :], op=ALU.mult),
                             r=[pab, in_b[2]], w=[t1b])
                        P.op("dve", lambda e, t2=t2, pb2=pb2, m=m: e.tensor_tensor(out=t2, in0=pb2[:, :], in1=gB_t[:, m, :], op=ALU.mult),
                             r=[pbb, in_b[3]], w=[t2b])
                        P.op("pool", lambda e, t1=t1, t2=t2, m=m: e.tensor_tensor(out=mixT[:, m, :], in0=t1, in1=t2, op=ALU.add),
                             r=[t1b, t2b], w=[mix_b] + hT_b)
                for cg in range(4):
                    wo, wob = wring.get()
                    P.dma("sp", wo, wb_out.rearrange("(c p) n -> p c n", p=128)[:, :, cg * 512:(cg + 1) * 512],
                          r=[DB(("wb_out", c)) for c in range(16)], w=[wob], chan=wob)
                    for j in range(4):
                        pt, ptb = mmbank()
                        for m in range(16):
                            P.op("pe", lambda e, pt=pt, m=m, j=j, wo=wo: e.matmul(pt[:, :], lhsT=mixT[:, m, j * 128:(j + 1) * 128], rhs=wo[:, m, :],
                                                                               start=(m == 0), stop=(m == 15)), r=[wob, mix_b], w=[ptb])
                        P.op("dve", lambda e, pt=pt, j=j, cg=cg: e.tensor_tensor(out=ht[j][:, cg * 512:(cg + 1) * 512], in0=pt[:, :],
                                                                               in1=ht[j][:, cg * 512:(cg + 1) * 512], op=ALU.add),
                             r=[ptb, h_b[j]], w=[h_b[j]])
                for j in range(4):
                    ss, ssb = rstd(ht[j], h_b[j])
                    hn, hnb = hnr.get()
                    P.op("dve", lambda e, j=j, ss=ss, hn=hn: e.scalar_tensor_tensor(out=hn, in0=ht[j], scalar=ss[:, 1:2], in1=wmlp,
                                                                                     op0=ALU.mult, op1=ALU.mult),
                         r=[h_b[j], ssb, cst_b], w=[hnb])
                    for hb in range(2):
                        tp = psum[6 + hb].bitcast(BF16)
                        tpb = psb[6 + hb]
                        for kk in range(8):
                            k = hb * 8 + kk
                            P.op("pe", lambda e, tp=tp, kk=kk, k=k, hn=hn: e.transpose(out=tp[:, kk * 128:(kk + 1) * 128],
                                                                                     in_=hn[:, k * 128:(k + 1) * 128], identity=ident),
                                 r=[hnb, ident_b], w=[tpb])
                        dst = hnT[:, hb * 8:(hb + 1) * 8, j * 128:(j + 1) * 128]
                        src = tp.rearrange("p (a b) -> p a b", b=128)
                        if hb == 0:
                            P.op("act", lambda e, dst=dst, src=src: e.copy(out=dst, in_=src), r=[tpb], w=[hT_b[j], mix_b])
                        else:
                            P.op("dve", lambda e, dst=dst, src=src: e.tensor_copy(out=dst, in_=src), r=[tpb], w=[hT_b[j], mix_b])
                for fg in range(16):
                    wu, wub = wring.get()
                    P.dma("sp", wu, wb_up.rearrange("(c p) n -> p c n", p=128)[:, :, fg * 512:(fg + 1) * 512],
                          r=[DB(("wb_up", c)) for c in range(16)], w=[wub], chan=wub)
                    for c in range(4):
                        pt, ptb = mmbank()
                        for k in range(16):
                            P.op("pe", lambda e, pt=pt, k=k, c=c, wu=wu: e.matmul(pt[:, :], lhsT=wu[:, k, c * 128:(c + 1) * 128], rhs=hnT[:, k, :],
                                                                               start=(k == 0), stop=(k == 15)), r=[wub] + hT_b, w=[ptb])
                        t1, t1b = tmpr.get()
                        P.op("act", lambda e, t1=t1, pt=pt: e.activation(out=t1, in_=pt[:, :], func=AF.Relu), r=[ptb], w=[t1b])
                        f = fg * 4 + c
                        P.op("pool", lambda e, t1=t1, f=f: e.tensor_tensor(out=aT[:, f, :], in0=t1, in1=t1, op=ALU.mult),
                             r=[t1b], w=[aT_b] + in_b)
                for cg in range(4):
                    banks = [mmbank() for _ in range(4)]
                    for fb in range(4):
                        wd, wdb = wring.get()
                        P.dma("sp", wd, wb_dn.rearrange("(c p) n -> p c n", p=128)[:, fb * 16:(fb + 1) * 16, cg * 512:(cg + 1) * 512],
                              r=[DB(("wb_dn", c)) for c in range(fb * 16, fb * 16 + 16)], w=[wdb], chan=wdb)
                        for j in range(4):
                            pt, ptb = banks[j]
                            for i in range(16):
                                f = fb * 16 + i
                                P.op("pe", lambda e, pt=pt, f=f, i=i, j=j, wd=wd: e.matmul(pt[:, :], lhsT=aT[:, f, j * 128:(j + 1) * 128], rhs=wd[:, i, :],
                                                                                        start=(f == 0), stop=(f == 63)), r=[wdb, aT_b], w=[ptb])
                    for j in range(4):
                        pt, ptb = banks[j]
                        P.op("dve", lambda e, pt=pt, j=j, cg=cg: e.tensor_tensor(out=ht[j][:, cg * 512:(cg + 1) * 512], in0=pt[:, :],
                                                                               in1=ht[j][:, cg * 512:(cg + 1) * 512], op=ALU.add),
                             r=[ptb, h_b[j]], w=[h_b[j]])
                for j in range(4):
                    ss, ssb = rstd(ht[j], h_b[j])
                    P.op("dve", lambda e, j=j, ss=ss: e.scalar_tensor_tensor(out=ht[j], in0=ht[j], scalar=ss[:, 1:2], in1=wfin,
                                                                             op0=ALU.mult, op1=ALU.mult),
                         r=[h_b[j], ssb, cst_b], w=[h_b[j]])
                    P.dma("pool", out[t0 + j * 128:t0 + (j + 1) * 128, :], ht[j], r=[h_b[j]], chan=h_b[j])


        def phase3(nb=8):
            AR.reset()
            T2 = AR.alloc([NH * 7, 128], F32, "T2")
            cb = Buf("c3")
            P.dma("sp", T2, att_t2.rearrange("p h d q -> p (h d) q"), w=[cb], chan=cb)
            ind = AR.alloc([128], F32, "ind")
            P.dma("sp", ind[0:2, :], att_ind[:, :], w=[cb], chan=cb)
            ones_bf = AR.alloc([128], BF16, "ones")
            P.op("pool", lambda e: e.memset(ones_bf, 1.0), w=[cb])
            qring = Ring(AR, 2, [TOK], BF16, "qh")
            kring = Ring(AR, 2, [TK], BF16, "kh")
            vring = Ring(AR, 2, [NT, 128], BF16, "vh")
            yring = Ring(AR, 2, [TOK], BF16, "yah")
            rmring = Ring(AR, 3, [7 * 128], F32, "rm")
            scring = Ring(AR, 3, [512], F32, "sc")
            ptring = Ring(AR, 3, [7, 128], BF16, "pT")
            recring = Ring(AR, 3, [128], F32, "rec")
            bi = [0]

            def bank():
                i = bi[0] % nb
                bi[0] += 1
                return psum[i], psb[i]

            scale = float(HD) ** -0.5
            for h in range(NH):
                qh, qb = qring.get()
                kh, kb = kring.get()
                vh, vb = vring.get()
                yh, yb = yring.get()
                P.dma("sp", qh, qT[h], r=[DB(("qT", h))], w=[qb], chan=qb)
                P.dma("sp", kh, kT[h], r=[DB(("kT", h))], w=[kb], chan=kb)
                P.dma("sp", vh, vA[:, h * 128:(h + 1) * 128].rearrange("(t p) c -> p t c", p=128), r=[DB(("vA", h))], w=[vb], chan=vb)
                for p in range(NP):
                    rm, rmb = rmring.get()
                    P.dma("sp", rm[0:2, :], att_rmv[:, p].rearrange("a d q -> a (d q)"), w=[rmb], chan=rmb)
                    tiles = [(t, t - p + 3) for t in range(max(0, p - 3), min(NT - 1, p + 3) + 1)]
                    pT, ptb = ptring.get()
                    for grp in (tiles[0:4], tiles[4:]):
                        if not grp:
                            continue
                        bk, bkb = bank()
                        for gi, (t, di) in enumerate(grp):
                            P.op("pe", lambda e, bk=bk, gi=gi, t=t, p=p, kh=kh, qh=qh: e.matmul(bk[:, gi * 128:(gi + 1) * 128], lhsT=kh[:, t * 128:(t + 1) * 128],
                                                                                           rhs=qh[:, p * 128:(p + 1) * 128], start=True, stop=False),
                                 r=[kb, qb], w=[bkb])
                            P.op("pe", lambda e, bk=bk, gi=gi, di=di, rm=rm: e.matmul(bk[:, gi * 128:(gi + 1) * 128], lhsT=ind[0:2, :],
                                                                                   rhs=rm[0:2, di * 128:(di + 1) * 128], start=False, stop=True),
                                 r=[cb, rmb], w=[bkb])
                        di0 = grp[0][1]
                        n = len(grp)
                        sc, scb = scring.get()
                        P.op("dve", lambda e, sc=sc, bk=bk, n=n, di0=di0, h=h: e.scalar_tensor_tensor(
                            out=sc[:, 0:n * 128], in0=bk[:, 0:n * 128], scalar=scale,
                            in1=T2[:, h * 7 + di0:h * 7 + di0 + n, :].rearrange("p a b -> p (a b)"), op0=ALU.mult, op1=ALU.add),
                            r=[bkb, cb], w=[scb])
                        P.op("act", lambda e, sc=sc, pT=pT, n=n, di0=di0: e.activation(
                            out=pT[:, di0:di0 + n, :].rearrange("p a b -> p (a b)"), in_=sc[:, 0:n * 128], func=AF.Exp),
                            r=[scb], w=[ptb])
                    bo, bob = bank()
                    for idx, (t, di) in enumerate(tiles):
                        P.op("pe", lambda e, bo=bo, t=t, di=di, idx=idx, vh=vh, pT=pT, nt=len(tiles): e.matmul(
                            bo[:, 0:128], lhsT=vh[:, t, :], rhs=pT[:, di, :], start=(idx == 0), stop=(idx == nt - 1)),
                            r=[vb, ptb], w=[bob])
                    for idx, (t, di) in enumerate(tiles):
                        P.op("pe", lambda e, bo=bo, di=di, idx=idx, pT=pT, nt=len(tiles): e.matmul(
                            bo[:, 128:256], lhsT=ones_bf, rhs=pT[:, di, :], start=(idx == 0), stop=(idx == nt - 1)),
                            r=[cb, ptb], w=[bob])
                    rec, recb = recring.get()
                    P.op("dve", lambda e, rec=rec, bo=bo: e.reciprocal(out=rec, in_=bo[:, 128:256]), r=[bob], w=[recb])
                    P.op("dve", lambda e, rec=rec, bo=bo, yh=yh, p=p: e.tensor_tensor(out=yh[:, p * 128:(p + 1) * 128], in0=bo[:, 0:128], in1=rec, op=ALU.mult),
                         r=[bob, recb], w=[yb])
                    yield
                P.dma("pool", yaT[h], yh, r=[yb], w=[DB(("yaT", "all"))], chan=yb)


        def phase1b(reset=True, bank0=0, nb=8):
            if reset:
                AR.reset()
            cb = Buf("c1b")
            cw = AR.alloc([24, 5], F32, "cw1b")
            P.dma("sp", cw, conv_w[:, :, :], w=[cb], chan=cb)
            ones_b = AR.alloc([128], BF16, "ones1b")
            P.op("pool", lambda e: e.memset(ones_b, 1.0), w=[cb])
            rawr = Ring(AR, 3, [516], F32, "raw1b")
            accr = Ring(AR, 3, [512], F32, "acc1b")
            sqr = Ring(AR, 2, [512], BF16, "sq1b")
            rnr = Ring(AR, 2, [512], F32, "rn1b")
            outr = Ring(AR, 3, [512], BF16, "out1b")
            bi = [0]
            for ch in range(24):
                ti = ch // 8
                ntile = NST if ti == 0 else 2 * NST
                for tt in range(ntile):
                    t0 = tt * 512
                    raw, rawb = rawr.get()
                    P.dma("sp", raw, dnraw[ch, :, t0:t0 + 516], r=[DB(("dnraw", ch))], w=[rawb], chan=rawb)
                    acc, accb = accr.get()
                    P.op("dve", lambda e, acc=acc, raw=raw, ch=ch: e.tensor_scalar_mul(out=acc, in0=raw[:, 0:512], scalar1=cw[:, ch, 0:1]),
                         r=[rawb, cb], w=[accb])
                    for j in range(1, 5):
                        P.op("dve", lambda e, acc=acc, raw=raw, ch=ch, j=j: e.scalar_tensor_tensor(
                            out=acc, in0=raw[:, j:j + 512], scalar=cw[:, ch, j:j + 1], in1=acc, op0=ALU.mult, op1=ALU.add),
                            r=[rawb, cb, accb], w=[accb])
                    ot_, otb_ = outr.get()
                    if ti == 2:
                        P.op("act", lambda e, acc=acc, ot_=ot_: e.activation(out=ot_, in_=acc, func=AF.Silu), r=[accb], w=[otb_])
                    else:
                        P.op("act", lambda e, acc=acc: e.activation(out=acc, in_=acc, func=AF.Silu), r=[accb], w=[accb])
                        sq, sqb = sqr.get()
                        P.op("pool", lambda e, sq=sq, acc=acc: e.tensor_tensor(out=sq, in0=acc, in1=acc, op=ALU.mult), r=[accb], w=[sqb])
                        pt, ptb = psum[bank0 + bi[0] % nb], psb[bank0 + bi[0] % nb]
                        bi[0] += 1
                        P.op("pe", lambda e, pt=pt, sq=sq: e.matmul(pt[:, :], lhsT=ones_b, rhs=sq, start=True, stop=True), r=[cb, sqb], w=[ptb])
                        rn, rnb = rnr.get()
                        P.op("dve", lambda e, rn=rn, pt=pt: e.tensor_scalar_add(out=rn, in0=pt[:, :], scalar1=1e-6), r=[ptb], w=[rnb])
                        P.op("act", lambda e, rn=rn: e.sqrt(out=rn, in_=rn), r=[rnb], w=[rnb])
                        P.op("dve", lambda e, rn=rn: e.reciprocal(out=rn, in_=rn), r=[rnb], w=[rnb])
                        if ti == 0:
                            P.op("dve", lambda e, ot_=ot_, acc=acc, rn=rn: e.scalar_tensor_tensor(out=ot_, in0=acc, scalar=float(HD) ** -0.5, in1=rn,
                                                                                               op0=ALU.mult, op1=ALU.mult), r=[accb, rnb], w=[otb_])
                        else:
                            P.op("dve", lambda e, ot_=ot_, acc=acc, rn=rn: e.tensor_tensor(out=ot_, in0=acc, in1=rn, op=ALU.mult), r=[accb, rnb], w=[otb_])
                    P.dma("pool", dnc[ch, :, t0:t0 + 512], ot_, r=[otb_], w=[DB(("dnc", ch))], chan=otb_)
                    yield

        def phase2():
            AR.reset()
            cb = Buf("c2")
            gm = AR.alloc([9, 128], F32, "gmask")
            P.dma("sp", gm, gmask[:, :, :], w=[cb], chan=cb)
            sel = AR.alloc([8, 128], F32, "sel")
            P.dma("sp", sel[0:8], gsel[:, :, :], w=[cb], chan=cb)
            identf = AR.alloc([128], F32, "identf")
            ident = AR.alloc([128], BF16, "ident")
            ones_f = AR.alloc([128], F32, "ones_f")
            P.op("pool", lambda e: e.memset(identf, 0.0), w=[cb])
            P.op("pool", lambda e: e.affine_select(out=identf, in_=identf, pattern=[[-1, 128]], compare_op=ALU.not_equal,
                                                   fill=1.0, base=0, channel_multiplier=1), r=[cb], w=[cb])
            P.op("dve", lambda e: e.tensor_copy(out=ident, in_=identf), r=[cb], w=[cb])
            P.op("pool", lambda e: e.memset(ones_f, 1.0), w=[cb])
            ones_b = AR.alloc([128], BF16, "ones_b")
            P.op("pool", lambda e: e.memset(ones_b, 1.0), w=[cb])
            cw = AR.alloc([24, 5], F32, "cw")
            P.dma("sp", cw, conv_w[:, :, :], w=[cb], chan=cb)
            dtb = AR.alloc([16], F32, "dtb")
            ea = AR.alloc([16], F32, "ea")
            onw = AR.alloc([128], F32, "onw")
            P.dma("sp", dtb, bass.AP(tensor=dt_bias, offset=0, ap=[[0, 128], [1, 16]]), w=[cb], chan=cb)
            P.dma("sp", ea, bass.AP(tensor=a_log, offset=0, ap=[[0, 128], [1, 16]]), w=[cb], chan=cb)
            P.dma("sp", onw, bass.AP(tensor=onorm_w, offset=0, ap=[[0, 128], [1, 128]]), w=[cb], chan=cb)
            P.op("act", lambda e: e.activation(out=ea, in_=ea, func=AF.Exp), r=[cb], w=[cb])
            St = [[AR.alloc([128], F32, "S") for _ in range(NH)] for _ in range(2)]
            Sb = [[AR.alloc([128], BF16, "Sb") for _ in range(NH)] for _ in range(2)]
            S_b = [[Buf("S%d%d" % (d_, h_)) for h_ in range(NH)] for d_ in range(2)]
            for d_ in range(2):
                for h_ in range(NH):
                    P.op("pool", lambda e, d_=d_, h_=h_: e.memset(St[d_][h_], 0.0), w=[S_b[d_][h_]])
                    P.op("pool", lambda e, d_=d_, h_=h_: e.memset(Sb[d_][h_], 0.0), w=[S_b[d_][h_]])
            bar = Ring(AR, 2, [32], F32, "ba")
            gt = Ring(AR, 2, [16, 16], F32, "gt")
            gcTr = Ring(AR, 2, [128], F32, "gcT")
            ldr = Ring(AR, 8, [3, 128], BF16, "ld")
            f32r = Ring(AR, 56, [256], F32, "f32t")
            bfr = Ring(AR, 240, [128], BF16, "bft")
            otile = Ring(AR, 2, [1024], F32, "otile")
            o2 = AR.alloc([1024], F32, "o2")
            o2_b = Buf("o2")
            zt = AR.alloc([1024], F32, "zt")
            zt_b = Buf("zt")
            ybf = AR.alloc([1024], BF16, "ybf")
            ybf_b = Buf("ybf")
            ybst = AR.alloc([1024], BF16, "ybst")
            ybst_b = Buf("ybst")
            ssg = AR.alloc([24], F32, "ssg")
            ssg_b = Buf("ssg")
            junk = AR.alloc([128], F32, "junk2")
            junk_b = Buf("junk2")
            bi = [0]
            ei = [0]
            sbi = [0, 0, 0, 0]

            def bank():
                i = bi[0] % 8
                bi[0] += 1
                return psum[i], psb[i]

            def copy_any(dst, src, r, w):
                P.op("act", lambda e: e.copy(out=dst, in_=src), r=r, w=w)

            def gdn_pass(u, dr, full):
                t0 = u * 128
                c0 = dr * 8
                gstage = getattr(cfg, "gstage", 9)
                if gstage < 0.1:
                    return
                bat, batb = bar.get()
                P.dma("sp", bat, ba[t0:t0 + 128, :], r=[DB(("ba", u))], w=[batb], chan=batb)
                G, Gb = gt.get()
                x1, e1, sp, g, e2, tt, beta, lnt = (G[:, i, :] for i in range(8))
                gcs = G[:, 8:10, :].rearrange("p a b -> p (a b)")
                eg, ekg, c1, negc, beg = (G[:, i, 0:8] for i in range(10, 15))
                glast = G[:, 15, :]
                P.op("dve", lambda e: e.tensor_tensor(out=x1, in0=bat[:, 16:32], in1=dtb, op=ALU.add), r=[batb, cb], w=[Gb])
                P.op("act", lambda e: e.activation(out=e1, in_=x1, func=AF.Exp), r=[Gb], w=[Gb])
                P.op("dve", lambda e: e.tensor_scalar_add(out=e1, in0=e1, scalar1=1.0), r=[Gb], w=[Gb])
                P.op("act", lambda e: e.activation(out=sp, in_=e1, func=AF.Ln), r=[Gb], w=[Gb])
                P.op("dve", lambda e: e.scalar_tensor_tensor(out=g, in0=sp, scalar=-1.0, in1=ea, op0=ALU.mult, op1=ALU.mult), r=[Gb, cb], w=[Gb])
                P.op("act", lambda e: e.activation(out=e2, in_=bat[:, 0:16], func=AF.Exp, scale=-1.0), r=[batb], w=[Gb])
                P.op("dve", lambda e: e.tensor_scalar_add(out=tt, in0=e2, scalar1=1.0), r=[Gb], w=[Gb])
                P.op("dve", lambda e: e.reciprocal(out=beta, in_=tt), r=[Gb], w=[Gb])
                P.op("act", lambda e: e.activation(out=lnt, in_=tt, func=AF.Ln), r=[Gb], w=[Gb])
                if gstage < 0.6:
                    return
                bg, bgb = bank()
                for i, mi in enumerate((dr, 2, 3, 4)):
                    P.op("pe", lambda e, i=i, mi=mi: e.matmul(bg[:, i * 8:(i + 1) * 8], lhsT=gm[:, mi, :], rhs=g[:, c0:c0 + 8], start=True, stop=True),
                         r=[cb, Gb], w=[bgb])
                P.op("dve", lambda e: e.tensor_copy(out=gcs, in_=bg[:, 0:32]), r=[bgb], w=[Gb])
                gc = gcs[:, 0:8]
                glt = gcs[:, 8:16]
                P.op("act", lambda e: e.activation(out=eg, in_=gc, func=AF.Exp), r=[Gb], w=[Gb])
                P.op("dve", lambda e: e.tensor_tensor(out=ekg, in0=glt, in1=gc, op=ALU.subtract), r=[Gb], w=[Gb])
                P.op("act", lambda e: e.activation(out=ekg, in_=ekg, func=AF.Exp), r=[Gb], w=[Gb])
                P.op("act", lambda e: e.activation(out=glast, in_=gcs[:, 16:32], func=AF.Exp), r=[Gb], w=[Gb])
                P.op("dve", lambda e: e.tensor_tensor(out=c1, in0=gc, in1=lnt[:, c0:c0 + 8], op=ALU.subtract), r=[Gb], w=[Gb])
                P.op("dve", lambda e: e.tensor_scalar_mul(out=negc, in0=gc, scalar1=-1.0), r=[Gb], w=[Gb])
                P.op("dve", lambda e: e.tensor_tensor(out=beg, in0=beta[:, c0:c0 + 8], in1=eg, op=ALU.mult), r=[Gb], w=[Gb])
                if gstage < 0.9:
                    return
                bt, btb = bank()
                P.op("pe", lambda e: e.matmul(bt[0:8, 0:128], lhsT=g[:, c0:c0 + 8], rhs=gm[:, dr, :], start=True, stop=True), r=[Gb, cb], w=[btb])
                gcT, gcTb = gcTr.get()
                P.op("act", lambda e: e.copy(out=gcT[0:8, :], in_=bt[0:8, 0:128]), r=[btb], w=[gcTb])
                gstage = getattr(cfg, "gstage", 9)
                if gstage < 2:
                    return
                if full:
                    ot, otb = otile.get()
                def head_gen(h):
                    strm = h % 4

                    def bank():
                        i = sbi[strm] % 2
                        sbi[strm] += 1
                        return psum[strm * 2 + i], psb[strm * 2 + i]

                    ld, ldb = ldr.get()
                    kn, knb = ld[:, 1, :], ldb
                    vbf, vbfb = ld[:, 2, :], ldb
                    P.dma("sp", kn, dnc[8 + h, :, t0:t0 + 128], r=[DB(("dnc", 8 + h))], w=[ldb], chan=ldb)
                    P.dma("sp", vbf, dnc[16 + h, :, t0:t0 + 128], r=[DB(("dnc", 16 + h))], w=[ldb], chan=ldb)
                    if full:
                        qn, qnb = ld[:, 0, :], ldb
                        P.dma("sp", qn, dnc[h, :, t0:t0 + 128], r=[DB(("dnc", h))], w=[ldb], chan=ldb)
                    yield
                    btp, btpb = bank()
                    tpv = btp.bitcast(BF16)
                    tpv = btp
                    P.op("pe", lambda e, tpv=tpv, kn=kn: e.matmul(tpv[:, 0:128], lhsT=kn, rhs=ident, start=True, stop=True), r=[knb, cb], w=[btpb])
                    P.op("pe", lambda e, tpv=tpv, vbf=vbf: e.matmul(tpv[:, 128:256], lhsT=vbf, rhs=ident, start=True, stop=True), r=[vbfb, cb], w=[btpb])
                    yield
                    kbg, kbgb = bfr.get()
                    kg, kgb = bfr.get()
                    vbt, vbtb = bfr.get()
                    P.op("dve", lambda e, kbg=kbg, tpv=tpv, h=h: e.tensor_scalar_mul(out=kbg, in0=tpv[:, 0:128], scalar1=beg[:, h:h + 1]), r=[btpb, Gb], w=[kbgb])
                    P.op("dve", lambda e, kg=kg, tpv=tpv, h=h: e.tensor_scalar_mul(out=kg, in0=tpv[:, 0:128], scalar1=ekg[:, h:h + 1]), r=[btpb, Gb], w=[kgb])
                    P.op("dve", lambda e, vbt=vbt, tpv=tpv, h=h: e.tensor_scalar_mul(out=vbt, in0=tpv[:, 128:256], scalar1=beta[:, c0 + h:c0 + h + 1]),
                         r=[btpb, Gb], w=[vbtb])
                    yield
                    if gstage < 3:
                        return
                    bk, bkb = bank()
                    P.op("pe", lambda e, bk=bk, kn=kn: e.matmul(bk[:, 0:128], lhsT=kn, rhs=kn, start=True, stop=True), r=[knb], w=[bkb])
                    if full:
                        P.op("pe", lambda e, bk=bk, kn=kn, qn=qn: e.matmul(bk[:, 128:256], lhsT=kn, rhs=qn, start=True, stop=True), r=[knb, qnb], w=[bkb])
                    yield
                    br, brb = bank()
                    nreg = 3 if full else 1
                    for ri in range(nreg):
                        P.op("pe", lambda e, br=br, ri=ri, h=h, gcT=gcT: e.matmul(br[:, ri * 128:(ri + 1) * 128], lhsT=sel[0:8, h, :], rhs=gcT[0:8, :],
                                                                               start=True, stop=(ri == 2)), r=[cb, gcTb], w=[brb])
                        if ri < 2:
                            mi = (5 + dr) if ri == 0 else (7 + dr)
                            P.op("pe", lambda e, br=br, ri=ri, mi=mi: e.matmul(br[:, ri * 128:(ri + 1) * 128], lhsT=identf, rhs=gm[:, mi, :],
                                                                            start=False, stop=True), r=[cb], w=[brb])
                    yield
                    EA, EAb = f32r.get()
                    P.op("act", lambda e, EA=EA, br=br, h=h: e.activation(out=EA[:, 0:128], in_=br[:, 0:128], func=AF.Exp, bias=c1[:, h:h + 1], scale=-1.0),
                         r=[brb, Gb], w=[EAb])
                    yield
                    Nn, Nb = bfr.get()
                    P.op("dve", lambda e, Nn=Nn, bk=bk, EA=EA: e.scalar_tensor_tensor(out=Nn, in0=bk[:, 0:128], scalar=-1.0, in1=EA[:, 0:128],
                                                                                   op0=ALU.mult, op1=ALU.mult), r=[bkb, EAb], w=[Nb])
                    if full:
                        EQ, EQb = f32r.get()
                        P.op("act", lambda e, EQ=EQ, br=br, h=h: e.activation(out=EQ[:, 0:128], in_=br[:, 128:256], func=AF.Exp, bias=negc[:, h:h + 1], scale=1.0),
                             r=[brb, Gb], w=[EQb])
                        P.op("act", lambda e, EQ=EQ, br=br: e.activation(out=EQ[:, 128:256], in_=br[:, 256:384], func=AF.Exp), r=[brb], w=[EQb])
                        qkT, qkTb = bfr.get()
                        P.op("dve", lambda e, qkT=qkT, bk=bk, EQ=EQ: e.tensor_tensor(out=qkT, in0=bk[:, 128:256], in1=EQ[:, 0:128], op=ALU.mult),
                             r=[bkb, EQb], w=[qkTb])
                        qgT, qgTb = bfr.get()
                        P.op("dve", lambda e, qgT=qgT, qn=qn, EQ=EQ: e.tensor_tensor(out=qgT, in0=qn, in1=EQ[:, 128:256], op=ALU.mult),
                             r=[qnb, EQb], w=[qgTb])
                    yield
                    if gstage < 4 or h >= getattr(cfg, "gheads", 99):
                        return
                    bm, bmb = bank()
                    bmv = bm.bitcast(BF16)
                    bmv = bm
                    P.op("pe", lambda e, bmv=bmv, Nn=Nn: e.matmul(bmv[:, 0:128], lhsT=Nn, rhs=ident, start=True, stop=True), r=[Nb, cb], w=[bmb])
                    yield
                    Mm, Mb = bfr.get()
                    P.op("act", lambda e, Mm=Mm, bmv=bmv: e.copy(out=Mm, in_=bmv[:, 0:128]), r=[bmb], w=[Mb])
                    yield
                    if gstage < 4.15:
                        return
                    Pm, Pb = bfr.get()
                    P.op("dve", lambda e, Pm=Pm, Mm=Mm: e.tensor_tensor(out=Pm, in0=Mm, in1=ident, op=ALU.add), r=[Mb, cb], w=[Pb])
                    X, Xb, Y, Yb = Nn, Nb, Mm, Mb
                    for lvl in range(5):
                        if gstage < 4.25 + 0.1 * lvl:
                            break
                        yield
                        bd, bdb = bank()
                        if lvl < 4:
                            P.op("pe", lambda e, bd=bd, X=X, Y=Y: e.matmul(bd[:, 0:128], lhsT=X, rhs=Y, start=True, stop=True), r=[Xb, Yb], w=[bdb])
                        P.op("pe", lambda e, bd=bd, X=X, Y=Y: e.matmul(bd[:, 128:256], lhsT=Y, rhs=X, start=True, stop=True), r=[Xb, Yb], w=[bdb])
                        yield
                        X2, X2b = bfr.get()
                        copy_any(X2, bd[:, 128:256], [bdb], [X2b])
                        if lvl < 4:
                            Y2, Y2b = bfr.get()
                            copy_any(Y2, bd[:, 0:128], [bdb], [Y2b])
                        else:
                            Y2, Y2b = None, None
                        yield
                        bp, bpb = bank()
                        P.op("pe", lambda e, bp=bp, X2=X2, Pm=Pm: e.matmul(bp[:, 0:128], lhsT=X2, rhs=Pm, start=True, stop=True), r=[X2b, Pb], w=[bpb])
                        yield
                        P2, P2b = bfr.get()
                        P.op("dve", lambda e, P2=P2, bp=bp, Pm=Pm: e.tensor_tensor(out=P2, in0=bp[:, 0:128], in1=Pm, op=ALU.add), r=[bpb, Pb], w=[P2b])
                        X, Xb, Y, Yb, Pm, Pb = X2, X2b, Y2, Y2b, P2, P2b
                    yield
                    if gstage < 5:
                        return
                    bu, bub = bank()
                    P.op("pe", lambda e, bu=bu, Pm=Pm, vbt=vbt: e.matmul(bu[:, 0:128], lhsT=Pm, rhs=vbt, start=True, stop=True), r=[Pb, vbtb], w=[bub])
                    P.op("pe", lambda e, bu=bu, Pm=Pm, kbg=kbg: e.matmul(bu[:, 128:256], lhsT=kbg, rhs=Pm, start=True, stop=True), r=[Pb, kbgb], w=[bub])
                    yield
                    uu, uub = f32r.get()
                    P.op("act", lambda e, uu=uu, bu=bu: e.copy(out=uu[:, 0:128], in_=bu[:, 0:128]), r=[bub], w=[uub])
                    wT, wTb = bfr.get()
                    P.op("act", lambda e, wT=wT, bu=bu: e.copy(out=wT, in_=bu[:, 128:256]), r=[bub], w=[wTb])
                    yield
                    vn, vnb = bfr.get()
                    S32, Sbf, Sbuf = St[dr][h], Sb[dr][h], S_b[dr][h]
                    for c in ((0, 1) if dr == 0 else (1, 0)):
                        r0 = c * 64
                        yield
                        bw, bwb = bank()
                        P.op("pe", lambda e, bw=bw, wT=wT, Sbf=Sbf: e.matmul(bw[:, 0:128], lhsT=wT, rhs=Sbf, start=True, stop=True), r=[wTb, Sbuf], w=[bwb])
                        yield
                        P.op("dve", lambda e, vn=vn, uu=uu, bw=bw, r0=r0: e.tensor_tensor(out=vn[r0:r0 + 64, :], in0=uu[r0:r0 + 64, 0:128],
                                                                                       in1=bw[r0:r0 + 64, 0:128], op=ALU.subtract),
                             r=[uub, bwb], w=[vnb])
                        if full:
                            P.op("pe", lambda e, bw=bw, qgT=qgT, Sbf=Sbf: e.matmul(bw[:, 128:256], lhsT=qgT, rhs=Sbf, start=True, stop=False),
                                 r=[qgTb, Sbuf], w=[bwb])
                            P.op("pe", lambda e, bw=bw, qkT=qkT, vn=vn, r0=r0: e.matmul(bw[:, 128:256], lhsT=qkT[r0:r0 + 64, :], rhs=vn[r0:r0 + 64, :],
                                                                                     start=False, stop=True), r=[qkTb, vnb], w=[bwb])
                            P.op("act", lambda e, ot=ot, bw=bw, r0=r0, h=h: e.copy(out=ot[r0:r0 + 64, h * 128:(h + 1) * 128], in_=bw[r0:r0 + 64, 128:256]),
                                 r=[bwb], w=[otb])
                        yield
                        P.op("pe", lambda e, bw=bw, kg=kg, vn=vn, r0=r0: e.matmul(bw[:, 256:384], lhsT=kg[r0:r0 + 64, :], rhs=vn[r0:r0 + 64, :],
                                                                               start=True, stop=True), r=[kgb, vnb], w=[bwb])
                        yield
                        P.op("dve", lambda e, S32=S32, bw=bw, c=c, h=h: e.scalar_tensor_tensor(out=S32, in0=S32, scalar=glast[:, c * 8 + h:c * 8 + h + 1],
                                                                                           in1=bw[:, 256:384], op0=ALU.mult, op1=ALU.add),
                             r=[bwb, Gb, Sbuf], w=[Sbuf])
                        yield
                        P.op("act", lambda e, S32=S32, Sbf=Sbf: e.copy(out=Sbf, in_=S32), r=[Sbuf], w=[Sbuf])
                def stream_gen(s4):
                    for h_ in (s4, s4 + 4):
                        yield from head_gen(h_)

                gens = [stream_gen(s4) for s4 in range(4)]
                while gens:
                    for g_ in list(gens):
                        try:
                            next(g_)
                        except StopIteration:
                            gens.remove(g_)
                if not full or gstage < 6:
                    return
                if dr == 0:
                    P.dma("pool", ofw[t0:t0 + 128, :], ot, r=[otb], w=[DB(("ofw", u))], chan=otb)
                    return
                P.dma("sp", o2, ofw[t0:t0 + 128, :], r=[DB(("ofw", u))], w=[o2_b], chan=o2_b)
                P.dma("sp", zt, zs[t0:t0 + 128, :], r=[DB(("zs", "all"))], w=[zt_b], chan=zt_b)
                P.op("dve", lambda e: e.tensor_tensor(out=o2, in0=o2, in1=ot, op=ALU.add), r=[otb, o2_b], w=[o2_b])
                for h in range(NH):
                    P.op("act", lambda e, h=h: e.activation(out=junk, in_=o2[:, h * 128:(h + 1) * 128], func=AF.Square, accum_out=ssg[:, h:h + 1]),
                         r=[o2_b], w=[junk_b, ssg_b])
                P.op("dve", lambda e: e.tensor_scalar(out=ssg[:, 8:16], in0=ssg[:, 0:8], scalar1=1.0 / HD, scalar2=RMS_EPS, op0=ALU.mult, op1=ALU.add),
                     r=[ssg_b], w=[ssg_b])
                P.op("act", lambda e: e.sqrt(out=ssg[:, 8:16], in_=ssg[:, 8:16]), r=[ssg_b], w=[ssg_b])
                P.op("dve", lambda e: e.reciprocal(out=ssg[:, 8:16], in_=ssg[:, 8:16]), r=[ssg_b], w=[ssg_b])
                for h in range(NH):
                    P.op("dve", lambda e, h=h: e.scalar_tensor_tensor(out=o2[:, h * 128:(h + 1) * 128], in0=o2[:, h * 128:(h + 1) * 128],
                                                                    scalar=ssg[:, 8 + h:9 + h], in1=onw, op0=ALU.mult, op1=ALU.mult),
                         r=[o2_b, ssg_b, cb], w=[o2_b])
                P.op("dve", lambda e: e.tensor_tensor(out=ybf, in0=o2, in1=zt, op=ALU.mult), r=[o2_b, zt_b], w=[ybf_b])
                for half_ in range(2):
                    by, byb = bank()
                    for hh in range(4):
                        h = half_ * 4 + hh
                        P.op("pe", lambda e, h=h, hh=hh, by=by: e.matmul(by[:, hh * 128:(hh + 1) * 128], lhsT=ybf[:, h * 128:(h + 1) * 128], rhs=ident,
                                                                      start=True, stop=True), r=[ybf_b, cb], w=[byb])
                    P.op("act", lambda e, by=by, half_=half_: e.copy(out=ybst[:, half_ * 512:(half_ + 1) * 512], in_=by[:, 0:512]), r=[byb], w=[ybst_b])
                P.dma("pool", ybT[:, :, t0:t0 + 128].rearrange("h p t -> p h t"), ybst.rearrange("p (h t) -> p h t", t=128),
                      r=[ybst_b], w=[DB(("ybT", "all"))], chan=ybst_b)

            for s_ in range(min(2 * NU, getattr(cfg, "gsteps", 10 ** 9))):
                if s_ < NU:
                    gdn_pass(s_, 0, True)
                ub = 2 * NU - 1 - s_
                gdn_pass(ub, 1, ub < NU)

        for wsrc, wdst, rows, key in ((w_proj_a, wb_pa, 1024, "wb_pa"), (w_proj_b, wb_pb, 1024, "wb_pb"),
                                      (w_out, wb_out, D, "wb_out"), (w_mlp_up, wb_up, D, "wb_up"),
                                      (w_mlp_down, wb_dn, DFF, "wb_dn")):
            cast_weight(wsrc, wdst, rows, key)
        if 1 in cfg.phases:
            phase1()
        P.barrier()
        def drive(gens):
            gens = list(gens)
            while gens:
                for g_ in list(gens):
                    try:
                        next(g_)
                    except StopIteration:
                        gens.remove(g_)

        if 3 in cfg.phases and 2 in cfg.phases:
            drive([phase3(6), phase1b(False, 6, 2)])
        elif 3 in cfg.phases:
            drive([phase3()])
        elif 2 in cfg.phases:
            drive([phase1b()])
        P.barrier()
        if 2 in cfg.phases:
            phase2()
        P.barrier()
        if 4 in cfg.phases:
            phase4()
        P.barrier()

        finals = [o for o in P.allops if o.chan is not None and o.fn is not None]
        info = P.emit(st, final_waits=finals[-64:] if False else finals)
        print("ops:", {e: len(P.ops[e]) for e in ENGS}, "incs/chans:", info)
    return nc


def core_inputs(inputs, b, half):
    f32 = np.float32
    x = inputs["x"]
    w_in = inputs["w_in"]
    m = {}
    xs = x[b] if half == 0 else x[b, ::-1]
    m["xs"] = np.ascontiguousarray(xs, dtype=f32)
    ba = w_in[:, C_BETA:C_BETA + 32]
    conv_w = inputs["conv_w"]
    a_log = inputs["a_log"]
    dt_bias = inputs["dt_bias"]
    if half == 1:
        ba = np.concatenate([ba[:, 8:16], ba[:, 0:8], ba[:, 24:32], ba[:, 16:24]], axis=1)
        conv_w = conv_w[::-1]
        a_log = a_log[::-1]
        dt_bias = dt_bias[::-1]
    m["w_ba"] = np.ascontiguousarray(ba, dtype=f32)
    m["conv_w"] = np.ascontiguousarray(np.asarray(conv_w).reshape(5, 24, 128).transpose(2, 1, 0), dtype=f32)
    m["a_log"] = np.ascontiguousarray(a_log, dtype=f32)
    m["dt_bias"] = np.ascontiguousarray(dt_bias, dtype=f32)
    S = x.shape[1]
    m.update(host_tables(inputs["rpb"], S, half))
    for k in ("norm_mix_w", "w_in", "onorm_w", "w_proj_a", "w_proj_b", "w_out", "norm_mlp_w",
              "w_mlp_up", "w_mlp_down", "norm_final_w"):
        m[k] = np.ascontiguousarray(inputs[k], dtype=f32)
    return m


_TAB_CACHE = {}


def host_tables(rpb, S, half):
    f32 = np.float32
    R_ = S // GRID_W
    TOK = S // 2
    NP = TOK // 128
    wr = min(8, R_)
    BIGN = -30000.0
    kc = np.arange(64)[:, None]
    qc = np.arange(64)[None, :]
    if half == 1:
        kco, qco = 63 - kc, 63 - qc
    else:
        kco, qco = kc, qc
    cs = np.clip(qco - 8, 0, GRID_W - 16)
    colok = (kco >= cs) & (kco < cs + 16)
    dc = np.clip(kco - qco + 15, 0, 30)
    t2 = np.full((128, NH, 7, 128), BIGN, f32)
    for di in range(7):
        for a in range(2):
            for qr in range(2):
                drf = 2 * (di - 3) + a - qr
                dro = -drf if half == 1 else drf
                if abs(dro) > 7:
                    continue
                blk = np.where(colok[None], rpb[:, dro + 7][:, dc], BIGN)
                t2[a * 64:(a + 1) * 64, :, di, qr * 64:(qr + 1) * 64] = blk.transpose(1, 0, 2)
    rmv = np.full((2, NP, 7, 128), -1.0e5, f32)
    for p in range(NP):
        for di in range(7):
            for a in range(2):
                for qr in range(2):
                    krf = 2 * (p + di - 3) + a
                    rf = 2 * p + qr
                    kro, ro = (R_ - 1 - krf, R_ - 1 - rf) if half == 1 else (krf, rf)
                    rs = min(max(ro - wr // 2, 0), R_ - wr)
                    if 0 <= kro < R_ and rs <= kro <= rs + wr - 1:
                        rmv[a, p, di, qr * 64:(qr + 1) * 64] = 0.0
    ind = np.zeros((2, 128), f32)
    ind[0, :64] = 1.0
    ind[1, 64:] = 1.0
    i = np.arange(128)[:, None]
    j = np.arange(128)[None, :]
    same = (i // 64) == (j // 64)
    gm = np.zeros((128, 9, 128), f32)
    gm[:, 0] = same & (i <= j)
    gm[:, 1] = same & (i >= j)
    gm[:, 2] = same
    gm[:, 3] = np.broadcast_to(i < 64, (128, 128))
    gm[:, 4] = np.broadcast_to(i >= 64, (128, 128))
    gm[:, 5] = np.where(same & (j < i), 0.0, 30000.0)
    gm[:, 6] = np.where(same & (j > i), 0.0, 30000.0)
    gm[:, 7] = np.where(same & (i <= j), 0.0, -30000.0)
    gm[:, 8] = np.where(same & (i >= j), 0.0, -30000.0)
    sel = np.zeros((8, 8, 128), f32)
    for k in range(8):
        sel[k, k, :] = 1.0
    return {"att_t2": t2, "att_rmv": rmv, "att_ind": ind, "gmask": gm, "gsel": sel}


_NC_CACHE = {}


def kernel(**inputs):
    x = inputs["x"]
    B, S, _ = x.shape
    if S not in _NC_CACHE:
        _NC_CACHE[S] = build(Cfg(S))
    nc = _NC_CACHE[S]
    in_maps = [core_inputs(inputs, c // 2, c % 2) for c in range(2 * B)]
    res = run_bass_kernel_spmd(nc, in_maps, core_ids=list(range(2 * B)))
    out = np.empty((B, S, D), np.float32)
    TOK = S // 2
    for c in range(2 * B):
        o = res.results[c]["out"]
        if c % 2 == 0:
            out[c // 2, :TOK] = o
        else:
            out[c // 2, TOK:] = o[::-1]
    return out
```

```python
import contextlib
import numpy as np
import concourse.bass as bass
import concourse.mybir as mybir
from concourse.bass_utils import run_bass_kernel_spmd

F32 = mybir.dt.float32
BF16 = mybir.dt.bfloat16
AF = mybir.ActivationFunctionType
ALU = mybir.AluOpType
AX = mybir.AxisListType

D = 2048
NH = 8
HD = 128
IN_W = 11296
DFF = 8192
GRID_W = 64
C_ATT_Q, C_ATT_K, C_ATT_V = 0, 1024, 2048
C_DN_Q, C_DN_K, C_DN_V = 3072, 4096, 5120
C_Z, C_BETA, C_ALPHA, C_GA, C_GB = 6144, 7168, 7184, 7200, 9248
RMS_EPS = 1e-6
NEG = -30000.0

ENGS = ("pe", "act", "dve", "pool", "sp")


class Buf:
    __slots__ = ("name", "w", "r", "sem", "cnt", "last_dma", "excl")

    def __init__(self, name="", excl=False):
        self.name = name
        self.excl = excl
        self.w = {}
        self.r = {}
        self.sem = None
        self.cnt = 0
        self.last_dma = None


class Op:
    __slots__ = ("eng", "fn", "deps", "need", "tok", "chan")


class Prog:
    def __init__(self, nc):
        self.nc = nc
        self.ops = {e: [] for e in ENGS}
        self.allops = []
        self.chans = []
        self.pending_dma = []
        self.last = {e: None for e in ENGS}
        self.snap = []

    def op(self, eng, fn, r=(), w=(), chan=None, extra=()):
        o = Op()
        o.eng = eng
        o.fn = fn
        o.need = False
        o.tok = None
        o.chan = chan
        deps = {}
        for d in extra:
            deps[id(d)] = d
        for b in r:
            for d in b.w.values():
                deps[id(d)] = d
            if b.excl:
                for k, d in b.r.items():
                    if k != eng:
                        deps[id(d)] = d
        for b in w:
            for d in b.r.values():
                deps[id(d)] = d
            for d in b.w.values():
                deps[id(d)] = d
        if chan is not None:
            if chan.last_dma is not None:
                deps[id(chan.last_dma)] = chan.last_dma
            chan.last_dma = o
            if chan.sem is None:
                chan.sem = True
                self.chans.append(chan)
            self.pending_dma.append(o)
        key = id(o) if chan is not None else eng
        for b in r:
            b.r[key] = o
        for b in w:
            b.w = {key: o}
            b.r = {}
        if eng == "pe" and chan is None:
            deps = {k: d for k, d in deps.items() if not (d.eng == "pe" and d.chan is None)}
        o.deps = [d for d in deps.values() if d.fn is not None]
        if fn is not None and fn.__closure__:
            self.snap.append((fn, [id(c.cell_contents) for c in fn.__closure__]))
        self.ops[eng].append(o)
        self.allops.append(o)
        if chan is None and fn is not None:
            self.last[eng] = o
        return o

    def dma(self, eng, out, in_, r=(), w=(), chan=None, **kw):
        assert chan is not None
        return self.op(eng, lambda e: e.dma_start(out=out, in_=in_, **kw), r=r, w=w, chan=chan)

    def barrier(self):
        deps = [o for o in self.last.values() if o is not None] + self.pending_dma
        self.pending_dma = []
        for e in ENGS:
            self.op(e, None, extra=deps)

    def emit(self, st, final_waits=()):
        nc = self.nc
        bad = set()
        for fn, ids in self.snap:
            for name, c, i in zip(fn.__code__.co_freevars, fn.__closure__, ids):
                if id(c.cell_contents) != i:
                    bad.add((fn.__code__.co_firstlineno, name))
        if bad:
            raise RuntimeError("late-binding closure bug(s): %s" % sorted(bad))
        for o in self.allops:
            for d in o.deps:
                d.need = True
        for o in final_waits:
            o.need = True
        cnt = {e: 0 for e in ENGS}
        for o in self.allops:
            if o.fn is None:
                continue
            if o.chan is None:
                if o.need:
                    cnt[o.eng] += 1
                    o.tok = (o.eng, cnt[o.eng])
            else:
                o.chan.cnt += 16
                o.tok = (o.chan, o.chan.cnt)
        esem = {e: st.enter_context(nc.semaphore("s_" + e)) for e in ENGS}
        for i, c in enumerate(self.chans):
            c.sem = st.enter_context(nc.semaphore("d%d" % i))
        block = st.enter_context(nc.Block())

        def run(ename, eng, extra=()):
            seen = {}

            def wait(d):
                k, v = d.tok
                sem = esem[k] if isinstance(k, str) else k.sem
                sk = id(sem)
                if seen.get(sk, 0) < v:
                    eng.wait_ge(sem, v)
                    seen[sk] = v

            for o in self.ops[ename]:
                for d in o.deps:
                    wait(d)
                if o.fn is None:
                    continue
                ins = o.fn(eng)
                if o.chan is not None:
                    ins.then_inc(o.chan.sem, 16)
                elif o.need:
                    ins.then_inc(esem[ename], 1)
            for d in extra:
                wait(d)

        @block.tensor
        def _(e):
            run("pe", e)

        @block.scalar
        def _(e):
            run("act", e)

        @block.vector
        def _(e):
            run("dve", e)

        @block.gpsimd
        def _(e):
            run("pool", e)

        @block.sync
        def _(e):
            run("sp", e, extra=final_waits)

        return cnt, len(self.chans)


class Arena:
    def __init__(self, t, nwords):
        self.t = t
        self.n = nwords
        self.off = 0

    def reset(self):
        self.off = 0

    def alloc(self, free_shape, dt, name=""):
        n = int(np.prod(free_shape))
        words = n if dt == F32 else (n + 1) // 2
        a = self.off
        self.off += words
        assert self.off <= self.n, ("SBUF arena overflow", name, self.off, self.n)
        ap = self.t[:, a:a + words]
        if dt != F32:
            ap = ap.bitcast(dt)
        if len(free_shape) == 2:
            ap = ap.rearrange("p (a b) -> p a b", b=free_shape[1])
        elif len(free_shape) == 3:
            ap = ap.rearrange("p (a b c) -> p a b c", b=free_shape[1], c=free_shape[2])
        return ap


class Ring:
    def __init__(self, arena, n, free_shape, dt, name):
        self.t = [arena.alloc(free_shape, dt, name) for _ in range(n)]
        self.b = [Buf("%s%d" % (name, i)) for i in range(n)]
        self.i = 0
        self.n = n

    def get(self):
        i = self.i % self.n
        self.i += 1
        return self.t[i], self.b[i]


class Cfg:
    def __init__(self, S, debug=False, phases=(1, 2, 3, 4)):
        self.S = S
        self.TOK = S // 2
        self.NU = self.TOK // 128
        self.NST = self.TOK // 512
        self.HALO = 256
        self.debug = debug
        self.phases = phases


def build(cfg):
    S, TOK, NU, NST = cfg.S, cfg.TOK, cfg.NU, cfg.NST
    nc = bass.Bass("TRN2", target_bir_lowering=False)
    dbg_kind = "ExternalOutput" if cfg.debug else "Internal"

    def din(name, shape, dt=F32):
        return nc.dram_tensor(name, list(shape), dt, kind="ExternalInput")

    def dscr(name, shape, dt, dbg=False):
        return nc.dram_tensor(name, list(shape), dt, kind=(dbg_kind if dbg else "Internal"))

    xs = din("xs", [S, D])
    norm_mix_w = din("norm_mix_w", [D])
    w_in = din("w_in", [D, IN_W])
    w_ba = din("w_ba", [D, 32])
    conv_w = din("conv_w", [128, 24, 5])
    a_log = din("a_log", [2, 8])
    dt_bias = din("dt_bias", [2, 8])
    onorm_w = din("onorm_w", [128])
    w_proj_a = din("w_proj_a", [1024, D])
    w_proj_b = din("w_proj_b", [1024, D])
    w_out = din("w_out", [D, D])
    norm_mlp_w = din("norm_mlp_w", [D])
    w_mlp_up = din("w_mlp_up", [D, DFF])
    w_mlp_down = din("w_mlp_down", [DFF, D])
    norm_final_w = din("norm_final_w", [D])
    out = nc.dram_tensor("out", [TOK, D], F32, kind="ExternalOutput")
    NP = TOK // 128
    NT = NP + 2
    att_t2 = din("att_t2", [128, NH, 7, 128])
    att_rmv = din("att_rmv", [2, NP, 7, 128])
    att_ind = din("att_ind", [2, 128])
    gmask = din("gmask", [128, 9, 128])
    gsel = din("gsel", [8, 8, 128])
    ofw = dscr("ofw", [TOK, 1024], F32)
    dnc = dscr("dnc", [24, 128, S], BF16)

    wb_in = dscr("wb_in", [D, IN_W], BF16)
    wb_ba = dscr("wb_ba", [D, 32], BF16)
    wb_pa = dscr("wb_pa", [1024, D], BF16)
    wb_pb = dscr("wb_pb", [1024, D], BF16)
    wb_out = dscr("wb_out", [D, D], BF16)
    wb_up = dscr("wb_up", [D, DFF], BF16)
    wb_dn = dscr("wb_dn", [DFF, D], BF16)
    TK = TOK + cfg.HALO
    qT = dscr("qT", [NH, 128, TOK], BF16, dbg=True)
    kT = dscr("kT", [NH, 128, TK], BF16, dbg=True)
    vA = dscr("vA", [TK, 1024], BF16, dbg=True)
    RAWW = S + 4
    dnraw = dscr("dnraw", [24, 128, RAWW], F32, dbg=True)
    zs = dscr("zs", [TOK, 1024], F32, dbg=True)
    ba = dscr("ba", [S, 32], F32, dbg=True)
    gA = dscr("gA", [16, 128, TOK], BF16, dbg=True)
    gB = dscr("gB", [16, 128, TOK], BF16, dbg=True)

    inj = getattr(cfg, "inject", False)
    ykind = "ExternalInput" if inj else dbg_kind
    yaT = nc.dram_tensor("yaT", [NH, 128, TOK], BF16, kind=(dbg_kind if getattr(cfg, "inject_b_only", False) else ykind))
    ybT = nc.dram_tensor("ybT", [NH, 128, TOK], BF16, kind=ykind)

    st = contextlib.ExitStack()
    with st:
        NW = 50 * 1024
        arena_t = st.enter_context(nc.sbuf_tensor("arena", [128, NW], F32))
        AR = Arena(arena_t, NW)
        psum = [st.enter_context(nc.psum_tensor("ps%d" % i, [128, 512], F32)) for i in range(8)]
        psb = [Buf("ps%d" % i, excl=True) for i in range(8)]
        P = Prog(nc)

        dram_bufs = {}

        def DB(key):
            if key not in dram_bufs:
                dram_bufs[key] = Buf(str(key))
            return dram_bufs[key]

        castch = [Buf("cast%d" % i) for i in range(4)]
        ci = [0]

        def cast_weight(src, dst, rows, key):
            for r0 in range(0, rows, 128):
                ch = castch[ci[0] % 4]
                ci[0] += 1
                P.dma("pool", dst[r0:r0 + 128, :], src[r0:r0 + 128, :], w=[DB((key, r0 // 128))], chan=ch)

        cast_weight(w_ba, wb_ba, D, "wb_ba")
        cast_weight(w_in, wb_in, D, "wb_in")

        def phase1():
            AR.reset()
            wbc = AR.alloc([D], F32, "wbc")
            wbc_b = Buf("wbc")
            P.dma("sp", wbc, bass.AP(tensor=norm_mix_w, offset=0, ap=[[0, 128], [1, D]]), w=[wbc_b], chan=wbc_b)
            ident = AR.alloc([128], BF16, "ident")
            identf = AR.alloc([128], F32, "identf")
            ident_b = Buf("ident")
            P.op("pool", lambda e: e.memset(identf, 0.0), w=[ident_b])
            P.op("pool", lambda e: e.affine_select(out=identf, in_=identf, pattern=[[-1, 128]],
                                                   compare_op=ALU.not_equal, fill=1.0, base=0,
                                                   channel_multiplier=1), r=[ident_b], w=[ident_b])
            P.op("dve", lambda e: e.tensor_copy(out=ident, in_=identf), r=[ident_b], w=[ident_b])
            zero_t = AR.alloc([4], F32, "zero")
            zero_b = Buf("zero")
            P.op("pool", lambda e: e.memset(zero_t, 0.0), w=[zero_b])
            for c in range(24):
                P.dma("sp", dnraw[c, :, 0:2], zero_t[:, 0:2], r=[zero_b], w=[DB(("rawpad", c, 0))], chan=zero_b)
                P.dma("sp", dnraw[c, :, S + 2:S + 4], zero_t[:, 2:4], r=[zero_b], w=[DB(("rawpad", c, 1))], chan=zero_b)

            xring = Ring(AR, 3, [D], F32, "x")
            junk = AR.alloc([D], BF16, "junk")
            junk_b = Buf("junk")
            ssr = Ring(AR, 4, [2], F32, "ss")
            xnbr = Ring(AR, 2, [D], BF16, "xnb")
            xnTr = [AR.alloc([16, 512], BF16, "xnT%d" % i) for i in range(2)]
            xnTb = [[Buf("xnT%d_%d" % (i, j)) for j in range(4)] for i in range(2)]
            wring = Ring(AR, 3, [16, 512], BF16, "wslab")
            stg = Ring(AR, 6, [512], F32, "stg")
            evi = [0]
            mmi = [0]

            def evac_engine():
                evi[0] += 1
                return "act" if evi[0] % 2 else "dve"

            def mmbank():
                i = mmi[0] % 6
                mmi[0] += 1
                return psum[i], psb[i]

            def evac(kind, dst, src, rb, wbuf):
                if kind == "copy":
                    eng = evac_engine()
                    if eng == "act":
                        P.op("act", lambda e: e.copy(out=dst, in_=src), r=rb, w=wbuf)
                    else:
                        P.op("dve", lambda e: e.tensor_copy(out=dst, in_=src), r=rb, w=wbuf)
                elif kind == "sigmoid":
                    P.op("act", lambda e: e.activation(out=dst, in_=src, func=AF.Sigmoid), r=rb, w=wbuf)
                elif kind == "silu":
                    P.op("act", lambda e: e.activation(out=dst, in_=src, func=AF.Silu), r=rb, w=wbuf)

            def load_wslab(col0, ncols):
                wt, wbf = wring.get()
                if col0 == C_BETA:
                    src = wb_ba.rearrange("(c p) n -> p c n", p=128)
                    rk = "wb_ba"
                else:
                    src = wb_in.rearrange("(c p) n -> p c n", p=128)[:, :, col0:col0 + ncols]
                    rk = "wb_in"
                P.dma("sp", wt[:, :, 0:ncols], src, r=[DB((rk, c)) for c in range(16)], w=[wbf], chan=wbf)
                return wt, wbf

            def fm_group(xi, col0, ncols, ntok, kind, odt, dst_fn, dkey):
                wt, wbf = load_wslab(col0, ncols)
                xT = xnTr[xi]
                for c in range(ncols // 128):
                    pt, pb = mmbank()
                    for k in range(16):
                        P.op("pe", lambda e, k=k, c=c, pt=pt: e.matmul(pt[:, 0:ntok], lhsT=wt[:, k, c * 128:(c + 1) * 128],
                                                                    rhs=xT[:, k, 0:ntok], start=(k == 0), stop=(k == 15)),
                             r=[wbf] + xnTb[xi], w=[pb])
                    sg, sgb = stg.get()
                    dst = sg[:, 0:ntok] if odt == F32 else sg.bitcast(BF16)[:, 0:ntok]
                    evac(kind, dst, pt[:, 0:ntok], [pb], [sgb])
                    P.dma("pool", dst_fn(c), dst, r=[sgb], w=[DB((dkey, col0 // 128 + c, "fm"))], chan=sgb)

            def tm_group(xi, col0, ncols, nsub, kind, odt, dst_fn, dkey):
                wt, wbf = load_wslab(col0, ncols)
                xT = xnTr[xi]
                for j in range(nsub):
                    pt, pb = mmbank()
                    for k in range(16):
                        P.op("pe", lambda e, k=k, j=j, pt=pt: e.matmul(pt[:, 0:ncols], lhsT=xT[:, k, j * 128:(j + 1) * 128],
                                                                    rhs=wt[:, k, 0:ncols], start=(k == 0), stop=(k == 15)),
                             r=[wbf, xnTb[xi][j]], w=[pb])
                    sg, sgb = stg.get()
                    dst = sg[:, 0:ncols] if odt == F32 else sg.bitcast(BF16)[:, 0:ncols]
                    evac(kind, dst, pt[:, 0:ncols], [pb], [sgb])
                    P.dma("pool", dst_fn(j), dst, r=[sgb], w=[DB((dkey, j, col0, "tm"))], chan=sgb)

            for sti in range(2 * NST):
                xi = sti % 2
                t0 = sti * 512
                own = sti < NST
                first_oth = sti == NST
                nsub = 4
                for j in range(nsub):
                    xt, xb_ = xring.get()
                    P.dma("sp", xt, xs[t0 + j * 128:t0 + (j + 1) * 128, :], w=[xb_], chan=xb_)
                    ss, ssb = ssr.get()
                    P.op("act", lambda e, xt=xt, ss=ss: e.activation(out=junk, in_=xt, func=AF.Square, accum_out=ss[:, 0:1]),
                         r=[xb_], w=[junk_b, ssb])
                    P.op("dve", lambda e, ss=ss: e.tensor_scalar(out=ss[:, 1:2], in0=ss[:, 0:1], scalar1=1.0 / D, scalar2=RMS_EPS,
                                                                 op0=ALU.mult, op1=ALU.add), r=[ssb], w=[ssb])
                    P.op("act", lambda e, ss=ss: e.sqrt(out=ss[:, 1:2], in_=ss[:, 1:2]), r=[ssb], w=[ssb])
                    P.op("dve", lambda e, ss=ss: e.reciprocal(out=ss[:, 1:2], in_=ss[:, 1:2]), r=[ssb], w=[ssb])
                    xnb, xnbb = xnbr.get()
                    P.op("dve", lambda e, xt=xt, ss=ss, xnb=xnb: e.scalar_tensor_tensor(out=xnb, in0=xt, scalar=ss[:, 1:2], in1=wbc,
                                                                                         op0=ALU.mult, op1=ALU.mult),
                         r=[xb_, ssb, wbc_b], w=[xnbb])
                    for hb in range(2):
                        tp = psum[6 + hb].bitcast(BF16)
                        tpb = psb[6 + hb]
                        for kk in range(8):
                            k = hb * 8 + kk
                            P.op("pe", lambda e, tp=tp, kk=kk, k=k, xnb=xnb: e.transpose(out=tp[:, kk * 128:(kk + 1) * 128],
                                                                                       in_=xnb[:, k * 128:(k + 1) * 128], identity=ident),
                                 r=[xnbb, ident_b], w=[tpb])
                        dst = xnTr[xi][:, hb * 8:(hb + 1) * 8, j * 128:(j + 1) * 128]
                        src = tp.rearrange("p (a b) -> p a b", b=128)
                        if hb == 0:
                            P.op("act", lambda e, dst=dst, src=src: e.copy(out=dst, in_=src), r=[tpb], w=[xnTb[xi][j]])
                        else:
                            P.op("dve", lambda e, dst=dst, src=src: e.tensor_copy(out=dst, in_=src), r=[tpb], w=[xnTb[xi][j]])
                if own:
                    for g in range(2):
                        fm_group(xi, C_ATT_Q + g * 512, 512, 512, "copy", BF16,
                                 lambda c, g=g: qT[g * 4 + c, :, t0:t0 + 512], "qT%d" % sti)
                if own or first_oth:
                    ntk = 512 if own else cfg.HALO
                    for g in range(2):
                        fm_group(xi, C_ATT_K + g * 512, 512, ntk, "copy", BF16,
                                 lambda c, g=g, ntk=ntk: kT[g * 4 + c, :, t0:t0 + ntk], "kT%d" % sti)
                    for g in range(2):
                        tm_group(xi, C_ATT_V + g * 512, 512, ntk // 128, "copy", BF16,
                                 lambda j, g=g: vA[t0 + j * 128:t0 + (j + 1) * 128, g * 512:(g + 1) * 512], "vA%d" % sti)
                    ntq = 512 if own else 128
                    for g in range(2):
                        fm_group(xi, C_DN_Q + g * 512, 512, ntq, "copy", F32,
                                 lambda c, g=g, ntq=ntq: dnraw[g * 4 + c, :, 2 + t0:2 + t0 + ntq], "dnq%d" % sti)
                for g in range(4):
                    fm_group(xi, C_DN_K + g * 512, 512, 512, "copy", F32,
                             lambda c, g=g: dnraw[8 + g * 4 + c, :, 2 + t0:2 + t0 + 512], "dnkv%d" % sti)
                tm_group(xi, C_BETA, 32, 4, "copy", F32,
                         lambda j: ba[t0 + j * 128:t0 + (j + 1) * 128, :], "ba%d" % sti)
                if own:
                    for g in range(2):
                        tm_group(xi, C_Z + g * 512, 512, 4, "silu", F32,
                                 lambda j, g=g: zs[t0 + j * 128:t0 + (j + 1) * 128, g * 512:(g + 1) * 512], "zs%d" % sti)
                    for g in range(4):
                        fm_group(xi, C_GA + g * 512, 512, 512, "sigmoid", BF16,
                                 lambda c, g=g: gA[g * 4 + c, :, t0:t0 + 512], "gA%d" % sti)
                    for g in range(4):
                        fm_group(xi, C_GB + g * 512, 512, 512, "sigmoid", BF16,
                                 lambda c, g=g: gB[g * 4 + c, :, t0:t0 + 512], "gB%d" % sti)


        def phase4():
            AR.reset()
            wmlp = AR.alloc([D], F32, "wmlp")
            wfin = AR.alloc([D], F32, "wfin")
            cst_b = Buf("cst4")
            P.dma("sp", wmlp, bass.AP(tensor=norm_mlp_w, offset=0, ap=[[0, 128], [1, D]]), w=[cst_b], chan=cst_b)
            P.dma("sp", wfin, bass.AP(tensor=norm_final_w, offset=0, ap=[[0, 128], [1, D]]), w=[cst_b], chan=cst_b)
            ident = AR.alloc([128], BF16, "ident")
            identf = AR.alloc([128], F32, "identf")
            ident_b = Buf("ident4")
            P.op("pool", lambda e: e.memset(identf, 0.0), w=[ident_b])
            P.op("pool", lambda e: e.affine_select(out=identf, in_=identf, pattern=[[-1, 128]],
                                                   compare_op=ALU.not_equal, fill=1.0, base=0,
                                                   channel_multiplier=1), r=[ident_b], w=[ident_b])
            P.op("dve", lambda e: e.tensor_copy(out=ident, in_=identf), r=[ident_b], w=[ident_b])
            aT = AR.alloc([64, 512], BF16, "aT")
            aT_b = Buf("aT")
            base = aT.rearrange("p a b -> p (a b)")
            ya_t = base[:, 0:4096].rearrange("p (a b) -> p a b", b=512)
            yb_t = base[:, 4096:8192].rearrange("p (a b) -> p a b", b=512)
            gA_t = base[:, 8192:16384].rearrange("p (a b) -> p a b", b=512)
            gB_t = base[:, 16384:24576].rearrange("p (a b) -> p a b", b=512)
            in_b = [Buf("ya_t"), Buf("yb_t"), Buf("gA_t"), Buf("gB_t")]
            mixT = AR.alloc([16, 512], BF16, "mixT")
            hnT = mixT
            mix_b = Buf("mixT")
            hT_b = [Buf("hnT%d" % j) for j in range(4)]
            ht = [AR.alloc([D], F32, "h%d" % j) for j in range(4)]
            h_b = [Buf("h%d" % j) for j in range(4)]
            hnr = Ring(AR, 2, [D], BF16, "hn")
            junk = AR.alloc([D], BF16, "junk4")
            junk_b = Buf("junk4")
            ssr = Ring(AR, 4, [2], F32, "ss4")
            wring = Ring(AR, 2, [16, 512], BF16, "wslab4")
            tmpr = Ring(AR, 3, [512], F32, "tmp4")
            mmi = [0]

            def mmbank():
                i = mmi[0] % 6
                mmi[0] += 1
                return psum[i], psb[i]

            def rstd(src_t, src_b):
                ss, ssb = ssr.get()
                P.op("act", lambda e: e.activation(out=junk, in_=src_t, func=AF.Square, accum_out=ss[:, 0:1]),
                     r=[src_b], w=[junk_b, ssb])
                P.op("dve", lambda e: e.tensor_scalar(out=ss[:, 1:2], in0=ss[:, 0:1], scalar1=1.0 / D, scalar2=RMS_EPS,
                                                      op0=ALU.mult, op1=ALU.add), r=[ssb], w=[ssb])
                P.op("act", lambda e: e.sqrt(out=ss[:, 1:2], in_=ss[:, 1:2]), r=[ssb], w=[ssb])
                P.op("dve", lambda e: e.reciprocal(out=ss[:, 1:2], in_=ss[:, 1:2]), r=[ssb], w=[ssb])
                return ss, ssb

            for tt in range(NST):
                t0 = tt * 512
                srcs = [yaT, ybT, gA, gB]
                dsts = [ya_t, yb_t, gA_t, gB_t]
                keys = ["yaT", "ybT", "gA", "gB"]
                for i in range(4):
                    nchunk = 8 if i < 2 else 16
                    P.dma("sp", dsts[i], srcs[i][:, :, t0:t0 + 512].rearrange("c p t -> p c t"),
                          r=[DB((keys[i], "all"))], w=[in_b[i], aT_b], chan=in_b[i])
                for j in range(4):
                    P.dma("sp", ht[j], xs[t0 + j * 128:t0 + (j + 1) * 128, :], w=[h_b[j]], chan=h_b[j])
                for mg in range(4):
                    wa, wab = wring.get()
                    P.dma("sp", wa[:, 0:8, :], wb_pa.rearrange("(c p) n -> p c n", p=128)[:, :, mg * 512:(mg + 1) * 512],
                          r=[DB(("wb_pa", c)) for c in range(8)], w=[wab], chan=wab)
                    P.dma("sp", wa[:, 8:16, :], wb_pb.rearrange("(c p) n -> p c n", p=128)[:, :, mg * 512:(mg + 1) * 512],
                          r=[DB(("wb_pb", c)) for c in range(8)], w=[wab], chan=wab)
                    for mm in range(4):
                        m = mg * 4 + mm
                        pa, pab = mmbank()
                        for k in range(8):
                            P.op("pe", lambda e, pa=pa, k=k, mm=mm, wa=wa: e.matmul(pa[:, :], lhsT=wa[:, k, mm * 128:(mm + 1) * 128], rhs=ya_t[:, k, :],
                                                                                 start=(k == 0), stop=(k == 7)), r=[wab, in_b[0]], w=[pab])
                        pb2, pbb = mmbank()
                        for k in range(8):
                            P.op("pe", lambda e, pb2=pb2, k=k, mm=mm, wa=wa: e.matmul(pb2[:, :], lhsT=wa[:, 8 + k, mm * 128:(mm + 1) * 128], rhs=yb_t[:, k, :],
                                                                                   start=(k == 0), stop=(k == 7)), r=[wab, in_b[1]], w=[pbb])
                        t1, t1b = tmpr.get()
                        t2, t2b = tmpr.get()
                        P.op("dve", lambda e, t1=t1, pa=pa, m=m: e.tensor_tensor(out=t1, in0=pa[:, :], in1=gA_t[:, m, :], op=ALU.mult),
                             r=[pab, in_b[2]], w=[t1b])
                        P.op("dve", lambda e, t2=t2, pb2=pb2, m=m: e.tensor_tensor(out=t2, in0=pb2[:, :], in1=gB_t[:, m, :], op=ALU.mult),
                             r=[pbb, in_b[3]], w=[t2b])
                        P.op("pool", lambda e, t1=t1, t2=t2, m=m: e.tensor_tensor(out=mixT[:, m, :], in0=t1, in1=t2, op=ALU.add),
                             r=[t1b, t2b], w=[mix_b] + hT_b)
                for cg in range(4):
                    wo, wob = wring.get()
                    P.dma("sp", wo, wb_out.rearrange("(c p) n -> p c n", p=128)[:, :, cg * 512:(cg + 1) * 512],
                          r=[DB(("wb_out", c)) for c in range(16)], w=[wob], chan=wob)
                    for j in range(4):
                        pt, ptb = mmbank()
                        for m in range(16):
                            P.op("pe", lambda e, pt=pt, m=m, j=j, wo=wo: e.matmul(pt[:, :], lhsT=mixT[:, m, j * 128:(j + 1) * 128], rhs=wo[:, m, :],
                                                                               start=(m == 0), stop=(m == 15)), r=[wob, mix_b], w=[ptb])
                        P.op("dve", lambda e, pt=pt, j=j, cg=cg: e.tensor_tensor(out=ht[j][:, cg * 512:(cg + 1) * 512], in0=pt[:, :],
                                                                               in1=ht[j][:, cg * 512:(cg + 1) * 512], op=ALU.add),
                             r=[ptb, h_b[j]], w=[h_b[j]])
                for j in range(4):
                    ss, ssb = rstd(ht[j], h_b[j])
                    hn, hnb = hnr.get()
                    P.op("dve", lambda e, j=j, ss=ss, hn=hn: e.scalar_tensor_tensor(out=hn, in0=ht[j], scalar=ss[:, 1:2], in1=wmlp,
                                                                                     op0=ALU.mult, op1=ALU.mult),
                         r=[h_b[j], ssb, cst_b], w=[hnb])
                    for hb in range(2):
                        tp = psum[6 + hb].bitcast(BF16)
                        tpb = psb[6 + hb]
                        for kk in range(8):
                            k = hb * 8 + kk
                            P.op("pe", lambda e, tp=tp, kk=kk, k=k, hn=hn: e.transpose(out=tp[:, kk * 128:(kk + 1) * 128],
                                                                                     in_=hn[:, k * 128:(k + 1) * 128], identity=ident),
                                 r=[hnb, ident_b], w=[tpb])
                        dst = hnT[:, hb * 8:(hb + 1) * 8, j * 128:(j + 1) * 128]
                        src = tp.rearrange("p (a b) -> p a b", b=128)
                        if hb == 0:
                            P.op("act", lambda e, dst=dst, src=src: e.copy(out=dst, in_=src), r=[tpb], w=[hT_b[j], mix_b])
                        else:
                            P.op("dve", lambda e, dst=dst, src=src: e.tensor_copy(out=dst, in_=src), r=[tpb], w=[hT_b[j], mix_b])
                for fg in range(16):
                    wu, wub = wring.get()
                    P.dma("sp", wu, wb_up.rearrange("(c p) n -> p c n", p=128)[:, :, fg * 512:(fg + 1) * 512],
                          r=[DB(("wb_up", c)) for c in range(16)], w=[wub], chan=wub)
                    for c in range(4):
                        pt, ptb = mmbank()
                        for k in range(16):
                            P.op("pe", lambda e, pt=pt, k=k, c=c, wu=wu: e.matmul(pt[:, :], lhsT=wu[:, k, c * 128:(c + 1) * 128], rhs=hnT[:, k, :],
                                                                               start=(k == 0), stop=(k == 15)), r=[wub] + hT_b, w=[ptb])
                        t1, t1b = tmpr.get()
                        P.op("act", lambda e, t1=t1, pt=pt: e.activation(out=t1, in_=pt[:, :], func=AF.Relu), r=[ptb], w=[t1b])
                        f = fg * 4 + c
                        P.op("pool", lambda e, t1=t1, f=f: e.tensor_tensor(out=aT[:, f, :], in0=t1, in1=t1, op=ALU.mult),
                             r=[t1b], w=[aT_b] + in_b)
                for cg in range(4):
                    banks = [mmbank() for _ in range(4)]
                    for fb in range(4):
                        wd, wdb = wring.get()
                        P.dma("sp", wd, wb_dn.rearrange("(c p) n -> p c n", p=128)[:, fb * 16:(fb + 1) * 16, cg * 512:(cg + 1) * 512],
                              r=[DB(("wb_dn", c)) for c in range(fb * 16, fb * 16 + 16)], w=[wdb], chan=wdb)
                        for j in range(4):
                            pt, ptb = banks[j]
                            for i in range(16):
                                f = fb * 16 + i
                                P.op("pe", lambda e, pt=pt, f=f, i=i, j=j, wd=wd: e.matmul(pt[:, :], lhsT=aT[:, f, j * 128:(j + 1) * 128], rhs=wd[:, i, :],
                                                                                        start=(f == 0), stop=(f == 63)), r=[wdb, aT_b], w=[ptb])
                    for j in range(4):
                        pt, ptb = banks[j]
                        P.op("dve", lambda e, pt=pt, j=j, cg=cg: e.tensor_tensor(out=ht[j][:, cg * 512:(cg + 1) * 512], in0=pt[:, :],
                                                                               in1=ht[j][:, cg * 512:(cg + 1) * 512], op=ALU.add),
                             r=[ptb, h_b[j]], w=[h_b[j]])
                for j in range(4):
                    ss, ssb = rstd(ht[j], h_b[j])
                    P.op("dve", lambda e, j=j, ss=ss: e.scalar_tensor_tensor(out=ht[j], in0=ht[j], scalar=ss[:, 1:2], in1=wfin,
                                                                             op0=ALU.mult, op1=ALU.mult),
                         r=[h_b[j], ssb, cst_b], w=[h_b[j]])
                    P.dma("pool", out[t0 + j * 128:t0 + (j + 1) * 128, :], ht[j], r=[h_b[j]], chan=h_b[j])


        def phase3(nb=8):
            AR.reset()
            T2 = AR.alloc([NH * 7, 128], F32, "T2")
            cb = Buf("c3")
            P.dma("sp", T2, att_t2.rearrange("p h d q -> p (h d) q"), w=[cb], chan=cb)
            ind = AR.alloc([128], F32, "ind")
            P.dma("sp", ind[0:2, :], att_ind[:, :], w=[cb], chan=cb)
            ones_bf = AR.alloc([128], BF16, "ones")
            P.op("pool", lambda e: e.memset(ones_bf, 1.0), w=[cb])
            qring = Ring(AR, 2, [TOK], BF16, "qh")
            kring = Ring(AR, 2, [TK], BF16, "kh")
            vring = Ring(AR, 2, [NT, 128], BF16, "vh")
            yring = Ring(AR, 2, [TOK], BF16, "yah")
            rmring = Ring(AR, 3, [7 * 128], F32, "rm")
            scring = Ring(AR, 3, [512], F32, "sc")
            ptring = Ring(AR, 3, [7, 128], BF16, "pT")
            recring = Ring(AR, 3, [128], F32, "rec")
            bi = [0]

            def bank():
                i = bi[0] % nb
                bi[0] += 1
                return psum[i], psb[i]

            scale = float(HD) ** -0.5
            for h in range(NH):
                qh, qb = qring.get()
                kh, kb = kring.get()
                vh, vb = vring.get()
                yh, yb = yring.get()
                P.dma("sp", qh, qT[h], r=[DB(("qT", h))], w=[qb], chan=qb)
                P.dma("sp", kh, kT[h], r=[DB(("kT", h))], w=[kb], chan=kb)
                P.dma("sp", vh, vA[:, h * 128:(h + 1) * 128].rearrange("(t p) c -> p t c", p=128), r=[DB(("vA", h))], w=[vb], chan=vb)
                for p in range(NP):
                    rm, rmb = rmring.get()
                    P.dma("sp", rm[0:2, :], att_rmv[:, p].rearrange("a d q -> a (d q)"), w=[rmb], chan=rmb)
                    dlo, dhi = (0, 3) if p == 0 else (-2, 2)
                    tiles = [(t, t - p + 3) for t in range(max(0, p + dlo), min(NT - 1, p + dhi) + 1)]
                    pT, ptb = ptring.get()
                    for grp in (tiles[0:4], tiles[4:]):
                        if not grp:
                            continue
                        bk, bkb = bank()
                        for gi, (t, di) in enumerate(grp):
                            P.op("pe", lambda e, bk=bk, gi=gi, t=t, p=p, kh=kh, qh=qh: e.matmul(bk[:, gi * 128:(gi + 1) * 128], lhsT=kh[:, t * 128:(t + 1) * 128],
                                                                                           rhs=qh[:, p * 128:(p + 1) * 128], start=True, stop=False),
                                 r=[kb, qb], w=[bkb])
                            P.op("pe", lambda e, bk=bk, gi=gi, di=di, rm=rm: e.matmul(bk[:, gi * 128:(gi + 1) * 128], lhsT=ind[0:2, :],
                                                                                   rhs=rm[0:2, di * 128:(di + 1) * 128], start=False, stop=True),
                                 r=[cb, rmb], w=[bkb])
                        di0 = grp[0][1]
                        n = len(grp)
                        sc, scb = scring.get()
                        P.op("dve", lambda e, sc=sc, bk=bk, n=n, di0=di0, h=h: e.scalar_tensor_tensor(
                            out=sc[:, 0:n * 128], in0=bk[:, 0:n * 128], scalar=scale,
                            in1=T2[:, h * 7 + di0:h * 7 + di0 + n, :].rearrange("p a b -> p (a b)"), op0=ALU.mult, op1=ALU.add),
                            r=[bkb, cb], w=[scb])
                        P.op("act", lambda e, sc=sc, pT=pT, n=n, di0=di0: e.activation(
                            out=pT[:, di0:di0 + n, :].rearrange("p a b -> p (a b)"), in_=sc[:, 0:n * 128], func=AF.Exp),
                            r=[scb], w=[ptb])
                    bo, bob = bank()
                    for idx, (t, di) in enumerate(tiles):
                        P.op("pe", lambda e, bo=bo, t=t, di=di, idx=idx, vh=vh, pT=pT, nt=len(tiles): e.matmul(
                            bo[:, 0:128], lhsT=vh[:, t, :], rhs=pT[:, di, :], start=(idx == 0), stop=(idx == nt - 1)),
                            r=[vb, ptb], w=[bob])
                    for idx, (t, di) in enumerate(tiles):
                        P.op("pe", lambda e, bo=bo, di=di, idx=idx, pT=pT, nt=len(tiles): e.matmul(
                            bo[:, 128:256], lhsT=ones_bf, rhs=pT[:, di, :], start=(idx == 0), stop=(idx == nt - 1)),
                            r=[cb, ptb], w=[bob])
                    rec, recb = recring.get()
                    P.op("dve", lambda e, rec=rec, bo=bo: e.reciprocal(out=rec, in_=bo[:, 128:256]), r=[bob], w=[recb])
                    P.op("dve", lambda e, rec=rec, bo=bo, yh=yh, p=p: e.tensor_tensor(out=yh[:, p * 128:(p + 1) * 128], in0=bo[:, 0:128], in1=rec, op=ALU.mult),
                         r=[bob, recb], w=[yb])
                    yield
                P.dma("pool", yaT[h], yh, r=[yb], w=[DB(("yaT", "all"))], chan=yb)


        def phase1b(reset=True, bank0=0, nb=8):
            if reset:
                AR.reset()
            cb = Buf("c1b")
            cw = AR.alloc([24, 5], F32, "cw1b")
            P.dma("sp", cw, conv_w[:, :, :], w=[cb], chan=cb)
            ones_b = AR.alloc([128], BF16, "ones1b")
            P.op("pool", lambda e: e.memset(ones_b, 1.0), w=[cb])
            rawr = Ring(AR, 3, [516], F32, "raw1b")
            accr = Ring(AR, 3, [512], F32, "acc1b")
            sqr = Ring(AR, 2, [512], BF16, "sq1b")
            rnr = Ring(AR, 2, [512], F32, "rn1b")
            outr = Ring(AR, 3, [512], BF16, "out1b")
            bi = [0]
            for ch in range(24):
                ti = ch // 8
                ntile = NST if ti == 0 else 2 * NST
                for tt in range(ntile):
                    t0 = tt * 512
                    raw, rawb = rawr.get()
                    P.dma("sp", raw, dnraw[ch, :, t0:t0 + 516], r=[DB(("dnraw", ch))], w=[rawb], chan=rawb)
                    acc, accb = accr.get()
                    P.op("dve", lambda e, acc=acc, raw=raw, ch=ch: e.tensor_scalar_mul(out=acc, in0=raw[:, 0:512], scalar1=cw[:, ch, 0:1]),
                         r=[rawb, cb], w=[accb])
                    for j in range(1, 5):
                        P.op("dve", lambda e, acc=acc, raw=raw, ch=ch, j=j: e.scalar_tensor_tensor(
                            out=acc, in0=raw[:, j:j + 512], scalar=cw[:, ch, j:j + 1], in1=acc, op0=ALU.mult, op1=ALU.add),
                            r=[rawb, cb, accb], w=[accb])
                    ot_, otb_ = outr.get()
                    if ti == 2:
                        P.op("act", lambda e, acc=acc, ot_=ot_: e.activation(out=ot_, in_=acc, func=AF.Silu), r=[accb], w=[otb_])
                    else:
                        P.op("act", lambda e, acc=acc: e.activation(out=acc, in_=acc, func=AF.Silu), r=[accb], w=[accb])
                        sq, sqb = sqr.get()
                        P.op("pool", lambda e, sq=sq, acc=acc: e.tensor_tensor(out=sq, in0=acc, in1=acc, op=ALU.mult), r=[accb], w=[sqb])
                        pt, ptb = psum[bank0 + bi[0] % nb], psb[bank0 + bi[0] % nb]
                        bi[0] += 1
                        P.op("pe", lambda e, pt=pt, sq=sq: e.matmul(pt[:, :], lhsT=ones_b, rhs=sq, start=True, stop=True), r=[cb, sqb], w=[ptb])
                        rn, rnb = rnr.get()
                        P.op("dve", lambda e, rn=rn, pt=pt: e.tensor_scalar_add(out=rn, in0=pt[:, :], scalar1=1e-6), r=[ptb], w=[rnb])
                        P.op("act", lambda e, rn=rn: e.sqrt(out=rn, in_=rn), r=[rnb], w=[rnb])
                        P.op("dve", lambda e, rn=rn: e.reciprocal(out=rn, in_=rn), r=[rnb], w=[rnb])
                        if ti == 0:
                            P.op("dve", lambda e, ot_=ot_, acc=acc, rn=rn: e.scalar_tensor_tensor(out=ot_, in0=acc, scalar=float(HD) ** -0.5, in1=rn,
                                                                                               op0=ALU.mult, op1=ALU.mult), r=[accb, rnb], w=[otb_])
                        else:
                            P.op("dve", lambda e, ot_=ot_, acc=acc, rn=rn: e.tensor_tensor(out=ot_, in0=acc, in1=rn, op=ALU.mult), r=[accb, rnb], w=[otb_])
                    P.dma("pool", dnc[ch, :, t0:t0 + 512], ot_, r=[otb_], w=[DB(("dnc", ch))], chan=otb_)
                    yield

        def phase2():
            AR.reset()
            cb = Buf("c2")
            gm = AR.alloc([9, 128], F32, "gmask")
            P.dma("sp", gm, gmask[:, :, :], w=[cb], chan=cb)
            sel = AR.alloc([8, 128], F32, "sel")
            P.dma("sp", sel[0:8], gsel[:, :, :], w=[cb], chan=cb)
            identf = AR.alloc([128], F32, "identf")
            ident = AR.alloc([128], BF16, "ident")
            ones_f = AR.alloc([128], F32, "ones_f")
            P.op("pool", lambda e: e.memset(identf, 0.0), w=[cb])
            P.op("pool", lambda e: e.affine_select(out=identf, in_=identf, pattern=[[-1, 128]], compare_op=ALU.not_equal,
                                                   fill=1.0, base=0, channel_multiplier=1), r=[cb], w=[cb])
            P.op("dve", lambda e: e.tensor_copy(out=ident, in_=identf), r=[cb], w=[cb])
            P.op("pool", lambda e: e.memset(ones_f, 1.0), w=[cb])
            gmb = AR.alloc([4, 128], BF16, "gmb")
            P.op("dve", lambda e: e.tensor_copy(out=gmb, in_=gm[:, 5:9, :]), r=[cb], w=[cb])
            ones_b = AR.alloc([128], BF16, "ones_b")
            P.op("pool", lambda e: e.memset(ones_b, 1.0), w=[cb])
            cw = AR.alloc([24, 5], F32, "cw")
            P.dma("sp", cw, conv_w[:, :, :], w=[cb], chan=cb)
            dtb = AR.alloc([16], F32, "dtb")
            ea = AR.alloc([16], F32, "ea")
            onw = AR.alloc([128], F32, "onw")
            P.dma("sp", dtb, bass.AP(tensor=dt_bias, offset=0, ap=[[0, 128], [1, 16]]), w=[cb], chan=cb)
            P.dma("sp", ea, bass.AP(tensor=a_log, offset=0, ap=[[0, 128], [1, 16]]), w=[cb], chan=cb)
            P.dma("sp", onw, bass.AP(tensor=onorm_w, offset=0, ap=[[0, 128], [1, 128]]), w=[cb], chan=cb)
            P.op("act", lambda e: e.activation(out=ea, in_=ea, func=AF.Exp), r=[cb], w=[cb])
            St = [[AR.alloc([128], F32, "S") for _ in range(NH)] for _ in range(2)]
            Sb = [[AR.alloc([128], BF16, "Sb") for _ in range(NH)] for _ in range(2)]
            S_b = [[Buf("S%d%d" % (d_, h_)) for h_ in range(NH)] for d_ in range(2)]
            for d_ in range(2):
                for h_ in range(NH):
                    P.op("pool", lambda e, d_=d_, h_=h_: e.memset(St[d_][h_], 0.0), w=[S_b[d_][h_]])
                    P.op("pool", lambda e, d_=d_, h_=h_: e.memset(Sb[d_][h_], 0.0), w=[S_b[d_][h_]])
            bar = Ring(AR, 2, [32], F32, "ba")
            gt = Ring(AR, 2, [16, 16], F32, "gt")
            gcTr = Ring(AR, 2, [128], F32, "gcT")
            ldr = Ring(AR, 8, [3, 128], BF16, "ld")
            f32r = Ring(AR, 56, [256], F32, "f32t")
            bfr = Ring(AR, 240, [128], BF16, "bft")
            otile = Ring(AR, 2, [1024], F32, "otile")
            o2 = AR.alloc([1024], F32, "o2")
            o2_b = Buf("o2")
            zt = AR.alloc([1024], F32, "zt")
            zt_b = Buf("zt")
            ybf = AR.alloc([1024], BF16, "ybf")
            ybf_b = Buf("ybf")
            ybst = AR.alloc([1024], BF16, "ybst")
            ybst_b = Buf("ybst")
            ssg = AR.alloc([24], F32, "ssg")
            ssg_b = Buf("ssg")
            junk = AR.alloc([128], F32, "junk2")
            junk_b = Buf("junk2")
            bi = [0]
            ei = [0]
            sbi = [0, 0, 0, 0]

            def bank():
                i = bi[0] % 8
                bi[0] += 1
                return psum[i], psb[i]

            def copy_any(dst, src, r, w):
                P.op("act", lambda e: e.copy(out=dst, in_=src), r=r, w=w)

            def gdn_pass(u, dr, full):
                t0 = u * 128
                c0 = dr * 8
                gstage = getattr(cfg, "gstage", 9)
                if gstage < 0.1:
                    return
                bat, batb = bar.get()
                P.dma("sp", bat, ba[t0:t0 + 128, :], r=[DB(("ba", u))], w=[batb], chan=batb)
                G, Gb = gt.get()
                x1, e1, sp, g, e2, tt, beta, lnt = (G[:, i, :] for i in range(8))
                gcs = G[:, 8:10, :].rearrange("p a b -> p (a b)")
                eg, ekg, c1, negc, beg = (G[:, i, 0:8] for i in range(10, 15))
                glast = G[:, 15, :]
                P.op("dve", lambda e: e.tensor_tensor(out=x1, in0=bat[:, 16:32], in1=dtb, op=ALU.add), r=[batb, cb], w=[Gb])
                P.op("act", lambda e: e.activation(out=e1, in_=x1, func=AF.Exp), r=[Gb], w=[Gb])
                P.op("dve", lambda e: e.tensor_scalar_add(out=e1, in0=e1, scalar1=1.0), r=[Gb], w=[Gb])
                P.op("act", lambda e: e.activation(out=sp, in_=e1, func=AF.Ln), r=[Gb], w=[Gb])
                P.op("dve", lambda e: e.scalar_tensor_tensor(out=g, in0=sp, scalar=-1.0, in1=ea, op0=ALU.mult, op1=ALU.mult), r=[Gb, cb], w=[Gb])
                P.op("act", lambda e: e.activation(out=e2, in_=bat[:, 0:16], func=AF.Exp, scale=-1.0), r=[batb], w=[Gb])
                P.op("dve", lambda e: e.tensor_scalar_add(out=tt, in0=e2, scalar1=1.0), r=[Gb], w=[Gb])
                P.op("dve", lambda e: e.reciprocal(out=beta, in_=tt), r=[Gb], w=[Gb])
                P.op("act", lambda e: e.activation(out=lnt, in_=tt, func=AF.Ln), r=[Gb], w=[Gb])
                if gstage < 0.6:
                    return
                bg, bgb = bank()
                for i, mi in enumerate((dr, 2, 3, 4)):
                    P.op("pe", lambda e, i=i, mi=mi: e.matmul(bg[:, i * 8:(i + 1) * 8], lhsT=gm[:, mi, :], rhs=g[:, c0:c0 + 8], start=True, stop=True),
                         r=[cb, Gb], w=[bgb])
                P.op("dve", lambda e: e.tensor_copy(out=gcs, in_=bg[:, 0:32]), r=[bgb], w=[Gb])
                gc = gcs[:, 0:8]
                glt = gcs[:, 8:16]
                P.op("act", lambda e: e.activation(out=eg, in_=gc, func=AF.Exp), r=[Gb], w=[Gb])
                P.op("dve", lambda e: e.tensor_tensor(out=ekg, in0=glt, in1=gc, op=ALU.subtract), r=[Gb], w=[Gb])
                P.op("act", lambda e: e.activation(out=ekg, in_=ekg, func=AF.Exp), r=[Gb], w=[Gb])
                P.op("act", lambda e: e.activation(out=glast, in_=gcs[:, 16:32], func=AF.Exp), r=[Gb], w=[Gb])
                P.op("dve", lambda e: e.tensor_tensor(out=c1, in0=gc, in1=lnt[:, c0:c0 + 8], op=ALU.subtract), r=[Gb], w=[Gb])
                P.op("dve", lambda e: e.tensor_scalar_mul(out=negc, in0=gc, scalar1=-1.0), r=[Gb], w=[Gb])
                P.op("dve", lambda e: e.tensor_tensor(out=beg, in0=beta[:, c0:c0 + 8], in1=eg, op=ALU.mult), r=[Gb], w=[Gb])
                if gstage < 0.9:
                    return
                bt, btb = bank()
                P.op("pe", lambda e: e.matmul(bt[0:8, 0:128], lhsT=g[:, c0:c0 + 8], rhs=gm[:, dr, :], start=True, stop=True), r=[Gb, cb], w=[btb])
                gcT, gcTb = gcTr.get()
                P.op("act", lambda e: e.copy(out=gcT[0:8, :], in_=bt[0:8, 0:128]), r=[btb], w=[gcTb])
                gstage = getattr(cfg, "gstage", 9)
                if gstage < 2:
                    return
                if full:
                    ot, otb = otile.get()
                def head_gen(h):
                    strm = h % 4

                    def bank():
                        i = sbi[strm] % 2
                        sbi[strm] += 1
                        return psum[strm * 2 + i], psb[strm * 2 + i]

                    ld, ldb = ldr.get()
                    kn, knb = ld[:, 1, :], ldb
                    vbf, vbfb = ld[:, 2, :], ldb
                    P.dma("sp", kn, dnc[8 + h, :, t0:t0 + 128], r=[DB(("dnc", 8 + h))], w=[ldb], chan=ldb)
                    P.dma("sp", vbf, dnc[16 + h, :, t0:t0 + 128], r=[DB(("dnc", 16 + h))], w=[ldb], chan=ldb)
                    if full:
                        qn, qnb = ld[:, 0, :], ldb
                        P.dma("sp", qn, dnc[h, :, t0:t0 + 128], r=[DB(("dnc", h))], w=[ldb], chan=ldb)
                    yield
                    btp, btpb = bank()
                    tpv = btp.bitcast(BF16)
                    tpv = btp
                    P.op("pe", lambda e, tpv=tpv, kn=kn: e.matmul(tpv[:, 0:128], lhsT=kn, rhs=ident, start=True, stop=True), r=[knb, cb], w=[btpb])
                    P.op("pe", lambda e, tpv=tpv, vbf=vbf: e.matmul(tpv[:, 128:256], lhsT=vbf, rhs=ident, start=True, stop=True), r=[vbfb, cb], w=[btpb])
                    yield
                    kbg, kbgb = bfr.get()
                    kg, kgb = bfr.get()
                    vbt, vbtb = bfr.get()
                    P.op("dve", lambda e, kbg=kbg, tpv=tpv, h=h: e.tensor_scalar_mul(out=kbg, in0=tpv[:, 0:128], scalar1=beg[:, h:h + 1]), r=[btpb, Gb], w=[kbgb])
                    P.op("dve", lambda e, kg=kg, tpv=tpv, h=h: e.tensor_scalar_mul(out=kg, in0=tpv[:, 0:128], scalar1=ekg[:, h:h + 1]), r=[btpb, Gb], w=[kgb])
                    P.op("dve", lambda e, vbt=vbt, tpv=tpv, h=h: e.tensor_scalar_mul(out=vbt, in0=tpv[:, 128:256], scalar1=beta[:, c0 + h:c0 + h + 1]),
                         r=[btpb, Gb], w=[vbtb])
                    yield
                    if gstage < 3:
                        return
                    bk, bkb = bank()
                    P.op("pe", lambda e, bk=bk, kn=kn: e.matmul(bk[:, 0:128], lhsT=kn, rhs=kn, start=True, stop=True), r=[knb], w=[bkb])
                    if full:
                        P.op("pe", lambda e, bk=bk, kn=kn, qn=qn: e.matmul(bk[:, 128:256], lhsT=kn, rhs=qn, start=True, stop=True), r=[knb, qnb], w=[bkb])
                    yield
                    br, brb = bank()
                    nreg = 3 if full else 1
                    for ri in range(nreg):
                        P.op("pe", lambda e, br=br, ri=ri, h=h, gcT=gcT: e.matmul(br[:, ri * 128:(ri + 1) * 128], lhsT=sel[0:8, h, :], rhs=gcT[0:8, :],
                                                                               start=True, stop=(ri == 2)), r=[cb, gcTb], w=[brb])
                        if ri < 2:
                            mi = (5 + dr) if ri == 0 else (7 + dr)
                            P.op("pe", lambda e, br=br, ri=ri, mi=mi: e.matmul(br[:, ri * 128:(ri + 1) * 128], lhsT=ident, rhs=gmb[:, mi - 5, :],
                                                                            start=False, stop=True), r=[cb], w=[brb])
                    yield
                    EA, EAb = f32r.get()
                    P.op("act", lambda e, EA=EA, br=br, h=h: e.activation(out=EA[:, 0:128], in_=br[:, 0:128], func=AF.Exp, bias=c1[:, h:h + 1], scale=-1.0),
                         r=[brb, Gb], w=[EAb])
                    yield
                    Nn, Nb = bfr.get()
                    P.op("dve", lambda e, Nn=Nn, bk=bk, EA=EA: e.scalar_tensor_tensor(out=Nn, in0=bk[:, 0:128], scalar=-1.0, in1=EA[:, 0:128],
                                                                                   op0=ALU.mult, op1=ALU.mult), r=[bkb, EAb], w=[Nb])
                    if full:
                        EQ, EQb = f32r.get()
                        P.op("act", lambda e, EQ=EQ, br=br, h=h: e.activation(out=EQ[:, 0:128], in_=br[:, 128:256], func=AF.Exp, bias=negc[:, h:h + 1], scale=1.0),
                             r=[brb, Gb], w=[EQb])
                        P.op("act", lambda e, EQ=EQ, br=br: e.activation(out=EQ[:, 128:256], in_=br[:, 256:384], func=AF.Exp), r=[brb], w=[EQb])
                        qkT, qkTb = bfr.get()
                        P.op("dve", lambda e, qkT=qkT, bk=bk, EQ=EQ: e.tensor_tensor(out=qkT, in0=bk[:, 128:256], in1=EQ[:, 0:128], op=ALU.mult),
                             r=[bkb, EQb], w=[qkTb])
                        qgT, qgTb = bfr.get()
                        P.op("dve", lambda e, qgT=qgT, qn=qn, EQ=EQ: e.tensor_tensor(out=qgT, in0=qn, in1=EQ[:, 128:256], op=ALU.mult),
                             r=[qnb, EQb], w=[qgTb])
                    yield
                    if gstage < 4 or h >= getattr(cfg, "gheads", 99):
                        return
                    bm, bmb = bank()
                    bmv = bm.bitcast(BF16)
                    bmv = bm
                    P.op("pe", lambda e, bmv=bmv, Nn=Nn: e.matmul(bmv[:, 0:128], lhsT=Nn, rhs=ident, start=True, stop=True), r=[Nb, cb], w=[bmb])
                    yield
                    Mm, Mb = bfr.get()
                    P.op("act", lambda e, Mm=Mm, bmv=bmv: e.copy(out=Mm, in_=bmv[:, 0:128]), r=[bmb], w=[Mb])
                    yield
                    if gstage < 4.15:
                        return
                    Pm, Pb = bfr.get()
                    P.op("dve", lambda e, Pm=Pm, Mm=Mm: e.tensor_tensor(out=Pm, in0=Mm, in1=ident, op=ALU.add), r=[Mb, cb], w=[Pb])
                    X, Xb, Y, Yb = Nn, Nb, Mm, Mb
                    for lvl in range(5):
                        if gstage < 4.25 + 0.1 * lvl:
                            break
                        yield
                        bd, bdb = bank()
                        if lvl < 4:
                            P.op("pe", lambda e, bd=bd, X=X, Y=Y: e.matmul(bd[:, 0:128], lhsT=X, rhs=Y, start=True, stop=True), r=[Xb, Yb], w=[bdb])
                        P.op("pe", lambda e, bd=bd, X=X, Y=Y: e.matmul(bd[:, 128:256], lhsT=Y, rhs=X, start=True, stop=True), r=[Xb, Yb], w=[bdb])
                        yield
                        X2, X2b = bfr.get()
                        copy_any(X2, bd[:, 128:256], [bdb], [X2b])
                        if lvl < 4:
                            Y2, Y2b = bfr.get()
                            copy_any(Y2, bd[:, 0:128], [bdb], [Y2b])
                        else:
                            Y2, Y2b = None, None
                        yield
                        bp, bpb = bank()
                        P.op("pe", lambda e, bp=bp, X2=X2, Pm=Pm: e.matmul(bp[:, 0:128], lhsT=X2, rhs=Pm, start=True, stop=True), r=[X2b, Pb], w=[bpb])
                        yield
                        P2, P2b = bfr.get()
                        P.op("dve", lambda e, P2=P2, bp=bp, Pm=Pm: e.tensor_tensor(out=P2, in0=bp[:, 0:128], in1=Pm, op=ALU.add), r=[bpb, Pb], w=[P2b])
                        X, Xb, Y, Yb, Pm, Pb = X2, X2b, Y2, Y2b, P2, P2b
                    yield
                    if gstage < 5:
                        return
                    bu, bub = bank()
                    P.op("pe", lambda e, bu=bu, Pm=Pm, vbt=vbt: e.matmul(bu[:, 0:128], lhsT=Pm, rhs=vbt, start=True, stop=True), r=[Pb, vbtb], w=[bub])
                    P.op("pe", lambda e, bu=bu, Pm=Pm, kbg=kbg: e.matmul(bu[:, 128:256], lhsT=kbg, rhs=Pm, start=True, stop=True), r=[Pb, kbgb], w=[bub])
                    yield
                    uu, uub = f32r.get()
                    P.op("act", lambda e, uu=uu, bu=bu: e.copy(out=uu[:, 0:128], in_=bu[:, 0:128]), r=[bub], w=[uub])
                    wT, wTb = bfr.get()
                    P.op("act", lambda e, wT=wT, bu=bu: e.copy(out=wT, in_=bu[:, 128:256]), r=[bub], w=[wTb])
                    yield
                    vn, vnb = bfr.get()
                    S32, Sbf, Sbuf = St[dr][h], Sb[dr][h], S_b[dr][h]
                    for c in ((0, 1) if dr == 0 else (1, 0)):
                        r0 = c * 64
                        yield
                        bw, bwb = bank()
                        P.op("pe", lambda e, bw=bw, wT=wT, Sbf=Sbf: e.matmul(bw[:, 0:128], lhsT=wT, rhs=Sbf, start=True, stop=True), r=[wTb, Sbuf], w=[bwb])
                        yield
                        P.op("dve", lambda e, vn=vn, uu=uu, bw=bw, r0=r0: e.tensor_tensor(out=vn[r0:r0 + 64, :], in0=uu[r0:r0 + 64, 0:128],
                                                                                       in1=bw[r0:r0 + 64, 0:128], op=ALU.subtract),
                             r=[uub, bwb], w=[vnb])
                        if full:
                            P.op("pe", lambda e, bw=bw, qgT=qgT, Sbf=Sbf: e.matmul(bw[:, 128:256], lhsT=qgT, rhs=Sbf, start=True, stop=False),
                                 r=[qgTb, Sbuf], w=[bwb])
                            P.op("pe", lambda e, bw=bw, qkT=qkT, vn=vn, r0=r0: e.matmul(bw[:, 128:256], lhsT=qkT[r0:r0 + 64, :], rhs=vn[r0:r0 + 64, :],
                                                                                     start=False, stop=True), r=[qkTb, vnb], w=[bwb])
                            P.op("act", lambda e, ot=ot, bw=bw, r0=r0, h=h: e.copy(out=ot[r0:r0 + 64, h * 128:(h + 1) * 128], in_=bw[r0:r0 + 64, 128:256]),
                                 r=[bwb], w=[otb])
                        yield
                        P.op("pe", lambda e, bw=bw, kg=kg, vn=vn, r0=r0: e.matmul(bw[:, 256:384], lhsT=kg[r0:r0 + 64, :], rhs=vn[r0:r0 + 64, :],
                                                                               start=True, stop=True), r=[kgb, vnb], w=[bwb])
                        yield
                        P.op("dve", lambda e, S32=S32, bw=bw, c=c, h=h: e.scalar_tensor_tensor(out=S32, in0=S32, scalar=glast[:, c * 8 + h:c * 8 + h + 1],
                                                                                           in1=bw[:, 256:384], op0=ALU.mult, op1=ALU.add),
                             r=[bwb, Gb, Sbuf], w=[Sbuf])
                        yield
                        P.op("act", lambda e, S32=S32, Sbf=Sbf: e.copy(out=Sbf, in_=S32), r=[Sbuf], w=[Sbuf])
                def stream_gen(s4):
                    for h_ in (s4, s4 + 4):
                        yield from head_gen(h_)

                gens = [stream_gen(s4) for s4 in range(4)]
                while gens:
                    for g_ in list(gens):
                        try:
                            next(g_)
                        except StopIteration:
                            gens.remove(g_)
                if not full or gstage < 6:
                    return
                if dr == 0:
                    P.dma("pool", ofw[t0:t0 + 128, :], ot, r=[otb], w=[DB(("ofw", u))], chan=otb)
                    return
                P.dma("sp", o2, ofw[t0:t0 + 128, :], r=[DB(("ofw", u))], w=[o2_b], chan=o2_b)
                P.dma("sp", zt, zs[t0:t0 + 128, :], r=[DB(("zs", "all"))], w=[zt_b], chan=zt_b)
                P.op("dve", lambda e: e.tensor_tensor(out=o2, in0=o2, in1=ot, op=ALU.add), r=[otb, o2_b], w=[o2_b])
                for h in range(NH):
                    P.op("act", lambda e, h=h: e.activation(out=junk, in_=o2[:, h * 128:(h + 1) * 128], func=AF.Square, accum_out=ssg[:, h:h + 1]),
                         r=[o2_b], w=[junk_b, ssg_b])
                P.op("dve", lambda e: e.tensor_scalar(out=ssg[:, 8:16], in0=ssg[:, 0:8], scalar1=1.0 / HD, scalar2=RMS_EPS, op0=ALU.mult, op1=ALU.add),
                     r=[ssg_b], w=[ssg_b])
                P.op("act", lambda e: e.sqrt(out=ssg[:, 8:16], in_=ssg[:, 8:16]), r=[ssg_b], w=[ssg_b])
                P.op("dve", lambda e: e.reciprocal(out=ssg[:, 8:16], in_=ssg[:, 8:16]), r=[ssg_b], w=[ssg_b])
                for h in range(NH):
                    P.op("dve", lambda e, h=h: e.scalar_tensor_tensor(out=o2[:, h * 128:(h + 1) * 128], in0=o2[:, h * 128:(h + 1) * 128],
                                                                    scalar=ssg[:, 8 + h:9 + h], in1=onw, op0=ALU.mult, op1=ALU.mult),
                         r=[o2_b, ssg_b, cb], w=[o2_b])
                P.op("dve", lambda e: e.tensor_tensor(out=ybf, in0=o2, in1=zt, op=ALU.mult), r=[o2_b, zt_b], w=[ybf_b])
                for half_ in range(2):
                    by, byb = bank()
                    for hh in range(4):
                        h = half_ * 4 + hh
                        P.op("pe", lambda e, h=h, hh=hh, by=by: e.matmul(by[:, hh * 128:(hh + 1) * 128], lhsT=ybf[:, h * 128:(h + 1) * 128], rhs=ident,
                                                                      start=True, stop=True), r=[ybf_b, cb], w=[byb])
                    P.op("act", lambda e, by=by, half_=half_: e.copy(out=ybst[:, half_ * 512:(half_ + 1) * 512], in_=by[:, 0:512]), r=[byb], w=[ybst_b])
                P.dma("pool", ybT[:, :, t0:t0 + 128].rearrange("h p t -> p h t"), ybst.rearrange("p (h t) -> p h t", t=128),
                      r=[ybst_b], w=[DB(("ybT", "all"))], chan=ybst_b)

            for s_ in range(min(2 * NU, getattr(cfg, "gsteps", 10 ** 9))):
                if s_ < NU:
                    gdn_pass(s_, 0, True)
                ub = 2 * NU - 1 - s_
                gdn_pass(ub, 1, ub < NU)

        for wsrc, wdst, rows, key in ((w_proj_a, wb_pa, 1024, "wb_pa"), (w_proj_b, wb_pb, 1024, "wb_pb"),
                                      (w_out, wb_out, D, "wb_out"), (w_mlp_up, wb_up, D, "wb_up"),
                                      (w_mlp_down, wb_dn, DFF, "wb_dn")):
            cast_weight(wsrc, wdst, rows, key)
        if 1 in cfg.phases:
            phase1()
        P.barrier()
        def drive(gens):
            gens = list(gens)
            while gens:
                for g_ in list(gens):
                    try:
                        next(g_)
                    except StopIteration:
                        gens.remove(g_)

        if 3 in cfg.phases and 2 in cfg.phases:
            drive([phase3(6), phase1b(False, 6, 2)])
        elif 3 in cfg.phases:
            drive([phase3()])
        elif 2 in cfg.phases:
            drive([phase1b()])
        P.barrier()
        if 2 in cfg.phases:
            phase2()
        P.barrier()
        if 4 in cfg.phases:
            phase4()
        P.barrier()

        finals = [o for o in P.allops if o.chan is not None and o.fn is not None]
        info = P.emit(st, final_waits=finals[-64:] if False else finals)
        print("ops:", {e: len(P.ops[e]) for e in ENGS}, "incs/chans:", info)
    return nc


def core_inputs(inputs, b, half):
    f32 = np.float32
    x = inputs["x"]
    w_in = inputs["w_in"]
    m = {}
    xs = x[b] if half == 0 else x[b, ::-1]
    m["xs"] = np.ascontiguousarray(xs, dtype=f32)
    ba = w_in[:, C_BETA:C_BETA + 32]
    conv_w = inputs["conv_w"]
    a_log = inputs["a_log"]
    dt_bias = inputs["dt_bias"]
    if half == 1:
        ba = np.concatenate([ba[:, 8:16], ba[:, 0:8], ba[:, 24:32], ba[:, 16:24]], axis=1)
        conv_w = conv_w[::-1]
        a_log = a_log[::-1]
        dt_bias = dt_bias[::-1]
    m["w_ba"] = np.ascontiguousarray(ba, dtype=f32)
    m["conv_w"] = np.ascontiguousarray(np.asarray(conv_w).reshape(5, 24, 128).transpose(2, 1, 0), dtype=f32)
    m["a_log"] = np.ascontiguousarray(a_log, dtype=f32)
    m["dt_bias"] = np.ascontiguousarray(dt_bias, dtype=f32)
    S = x.shape[1]
    m.update(host_tables(inputs["rpb"], S, half))
    for k in ("norm_mix_w", "w_in", "onorm_w", "w_proj_a", "w_proj_b", "w_out", "norm_mlp_w",
              "w_mlp_up", "w_mlp_down", "norm_final_w"):
        m[k] = np.ascontiguousarray(inputs[k], dtype=f32)
    return m


_TAB_CACHE = {}


def host_tables(rpb, S, half):
    f32 = np.float32
    R_ = S // GRID_W
    TOK = S // 2
    NP = TOK // 128
    wr = min(8, R_)
    BIGN = -30000.0
    kc = np.arange(64)[:, None]
    qc = np.arange(64)[None, :]
    if half == 1:
        kco, qco = 63 - kc, 63 - qc
    else:
        kco, qco = kc, qc
    cs = np.clip(qco - 8, 0, GRID_W - 16)
    colok = (kco >= cs) & (kco < cs + 16)
    dc = np.clip(kco - qco + 15, 0, 30)
    t2 = np.full((128, NH, 7, 128), BIGN, f32)
    for di in range(7):
        for a in range(2):
            for qr in range(2):
                drf = 2 * (di - 3) + a - qr
                dro = -drf if half == 1 else drf
                if abs(dro) > 7:
                    continue
                blk = np.where(colok[None], rpb[:, dro + 7][:, dc], BIGN)
                t2[a * 64:(a + 1) * 64, :, di, qr * 64:(qr + 1) * 64] = blk.transpose(1, 0, 2)
    rmv = np.full((2, NP, 7, 128), -1.0e5, f32)
    for p in range(NP):
        for di in range(7):
            for a in range(2):
                for qr in range(2):
                    krf = 2 * (p + di - 3) + a
                    rf = 2 * p + qr
                    kro, ro = (R_ - 1 - krf, R_ - 1 - rf) if half == 1 else (krf, rf)
                    rs = min(max(ro - wr // 2, 0), R_ - wr)
                    if 0 <= kro < R_ and rs <= kro <= rs + wr - 1:
                        rmv[a, p, di, qr * 64:(qr + 1) * 64] = 0.0
    ind = np.zeros((2, 128), f32)
    ind[0, :64] = 1.0
    ind[1, 64:] = 1.0
    i = np.arange(128)[:, None]
    j = np.arange(128)[None, :]
    same = (i // 64) == (j // 64)
    gm = np.zeros((128, 9, 128), f32)
    gm[:, 0] = same & (i <= j)
    gm[:, 1] = same & (i >= j)
    gm[:, 2] = same
    gm[:, 3] = np.broadcast_to(i < 64, (128, 128))
    gm[:, 4] = np.broadcast_to(i >= 64, (128, 128))
    gm[:, 5] = np.where(same & (j < i), 0.0, 30000.0)
    gm[:, 6] = np.where(same & (j > i), 0.0, 30000.0)
    gm[:, 7] = np.where(same & (i <= j), 0.0, -30000.0)
    gm[:, 8] = np.where(same & (i >= j), 0.0, -30000.0)
    sel = np.zeros((8, 8, 128), f32)
    for k in range(8):
        sel[k, k, :] = 1.0
    return {"att_t2": t2, "att_rmv": rmv, "att_ind": ind, "gmask": gm, "gsel": sel}


_NC_CACHE = {}


def kernel(**inputs):
    x = inputs["x"]
    B, S, _ = x.shape
    if S not in _NC_CACHE:
        _NC_CACHE[S] = build(Cfg(S))
    nc = _NC_CACHE[S]
    in_maps = [core_inputs(inputs, c // 2, c % 2) for c in range(2 * B)]
    res = run_bass_kernel_spmd(nc, in_maps, core_ids=list(range(2 * B)))
    out = np.empty((B, S, D), np.float32)
    TOK = S // 2
    for c in range(2 * B):
        o = res.results[c]["out"]
        if c % 2 == 0:
            out[c // 2, :TOK] = o
        else:
            out[c // 2, TOK:] = o[::-1]
    return out
```
